# Optimizing a Trainium2 kernel written in Bass

```python
import jax, jax.numpy as jnp
from jax import lax
import numpy as np

D_MODEL = 2048
BATCH = 4
SEQ = 2048
DEPTH = 1

HEAD_DIM = 128
N_ATTN_HEADS = 8
ATTN_W = N_ATTN_HEADS * HEAD_DIM
DILATED_PATTERNS = ((128, 1), (512, 4), (2048, 16))
BAND_BLOCK = 128
N_GLA_HEADS = 4
GLA_DK = 128
GLA_DV = 256
GLA_KW = N_GLA_HEADS * GLA_DK
GLA_VW = N_GLA_HEADS * GLA_DV
GLA_GATE_RANK = 16
GLA_TAU = 16.0
GLA_CHUNK = 64
MIX_COLS = 3 * ATTN_W + 2 * GLA_KW + 2 * GLA_VW + GLA_GATE_RANK
N_MEM = 256
N_XATTN_HEADS = 4
XATTN_W = N_XATTN_HEADS * HEAD_DIM
D_FF = 5632
ROPE_THETA = 10000.0
EPS = 1e-6
NEG = -1e30

kernel_name = "hybrid_dilated_gla_macaron_layer"


def rms_norm(x, g):
    xf = x.astype(jnp.float32)
    y = xf * lax.rsqrt(jnp.mean(xf * xf, axis=-1, keepdims=True) + EPS)
    return (y * g.astype(jnp.float32)).astype(x.dtype)


def rope(x, pos):
    half = x.shape[-1] // 2
    inv = ROPE_THETA ** (-jnp.arange(half, dtype=jnp.float32) / half)
    ang = pos.astype(jnp.float32)[:, None] * inv[None, :]
    cos, sin = jnp.cos(ang), jnp.sin(ang)
    x1 = x[..., :half].astype(jnp.float32)
    x2 = x[..., half:].astype(jnp.float32)
    return jnp.concatenate([x1 * cos - x2 * sin, x2 * cos + x1 * sin], axis=-1).astype(x.dtype)


def swiglu(h, w_gate, w_up, w_down):
    return (jax.nn.silu(h @ w_gate) * (h @ w_up)) @ w_down


def banded_causal_attention(q, k, v, n_back):
    B, H, L, D = q.shape
    nb = -(-L // BAND_BLOCK)
    Lp = nb * BAND_BLOCK
    pad = ((0, 0), (0, 0), (0, Lp - L), (0, 0))
    qb = jnp.pad(q, pad).reshape(B, H, nb, BAND_BLOCK, D)
    kb = jnp.pad(k, pad).reshape(B, H, nb, BAND_BLOCK, D)
    vb = jnp.pad(v, pad).reshape(B, H, nb, BAND_BLOCK, D)
    prev = ((0, 0), (0, 0), (1, 0), (0, 0), (0, 0))
    k2 = jnp.concatenate([jnp.pad(kb, prev)[:, :, :-1], kb], axis=3)
    v2 = jnp.concatenate([jnp.pad(vb, prev)[:, :, :-1], vb], axis=3)
    s = jnp.einsum("bhnqd,bhnkd->bhnqk", qb, k2, preferred_element_type=jnp.float32)
    qi = jnp.arange(BAND_BLOCK)[:, None]
    kj = jnp.arange(2 * BAND_BLOCK)[None, :]
    dist = BAND_BLOCK + qi - kj
    blk = jnp.arange(nb)[:, None, None]
    valid = (dist >= 0) & (dist <= n_back) & ((blk > 0) | (kj >= BAND_BLOCK))
    s = jnp.where(valid, s, NEG)
    m = jnp.max(s, axis=-1, keepdims=True)
    p = jnp.exp(s - m)
    den = jnp.sum(p, axis=-1, keepdims=True)
    o = jnp.einsum("bhnqk,bhnkd->bhnqd", p, v2.astype(jnp.float32)) / den
    lse = (m + jnp.log(den))[..., 0]
    o = o.reshape(B, H, Lp, D)[:, :, :L]
    lse = lse.reshape(B, H, Lp)[:, :, :L]
    return o, lse


def dilated_attention(q, k, v):
    B, H, S, D = q.shape
    outs, lses = [], []
    for w, r in DILATED_PATTERNS:
        def to_cls(t):
            return t.reshape(B, H, S // r, r, D).transpose(0, 1, 3, 2, 4).reshape(B, H * r, S // r, D)
        o, l = banded_causal_attention(to_cls(q), to_cls(k), to_cls(v), w // r)
        outs.append(o.reshape(B, H, r, S // r, D).transpose(0, 1, 3, 2, 4).reshape(B, H, S, D))
        lses.append(l.reshape(B, H, r, S // r).transpose(0, 1, 3, 2).reshape(B, H, S))
    wts = jax.nn.softmax(jnp.stack(lses, axis=0), axis=0)
    return jnp.einsum("pbhs,pbhsd->bhsd", wts, jnp.stack(outs, axis=0))


def gla_chunked(q, k, v, log_a):
    B, H, S, dk = q.shape
    dv = v.shape[-1]
    n = S // GLA_CHUNK
    q = q.reshape(B, H, n, GLA_CHUNK, dk)
    k = k.reshape(B, H, n, GLA_CHUNK, dk)
    v = v.reshape(B, H, n, GLA_CHUNK, dv)
    b = jnp.cumsum(log_a.reshape(B, H, n, GLA_CHUNK, dk), axis=3)
    q_e = q * jnp.exp(b)
    k_e = k * jnp.exp(-b)
    causal = jnp.tril(jnp.ones((GLA_CHUNK, GLA_CHUNK), dtype=bool))
    A = jnp.where(causal, jnp.einsum("bhnid,bhnjd->bhnij", q_e, k_e), 0.0)
    o_intra = jnp.einsum("bhnij,bhnjd->bhnid", A, v)
    b_last = b[:, :, :, -1]
    k_dec = k * jnp.exp(b_last[:, :, :, None, :] - b)
    dS = jnp.einsum("bhncd,bhnce->bhnde", k_dec, v)

    def step(state, inp):
        decay, ds = inp
        return decay[..., None] * state + ds, state

    init = jnp.zeros((B, H, dk, dv), jnp.float32)
    _, s_before = lax.scan(step, init, (jnp.moveaxis(jnp.exp(b_last), 2, 0), jnp.moveaxis(dS, 2, 0)))
    s_before = jnp.moveaxis(s_before, 0, 2)
    o_inter = jnp.einsum("bhncd,bhnde->bhnce", q_e, s_before)
    return (o_intra + o_inter).reshape(B, H, S, dv)


def hybrid_mixer(h, pos, w_in, attn_q_norm, attn_k_norm, gla_w_gate2, gla_b_gate2, gla_out_norm, w_out):
    B, S, _ = h.shape
    proj = h @ w_in
    splits = np.cumsum([ATTN_W, ATTN_W, ATTN_W, GLA_KW, GLA_KW, GLA_VW, GLA_VW])
    qa, ka, va, qg, kg, vg, rg, glr = jnp.split(proj, splits, axis=-1)

    def heads(t, nh):
        return t.reshape(B, S, nh, -1).transpose(0, 2, 1, 3)

    qa = rope(rms_norm(heads(qa, N_ATTN_HEADS), attn_q_norm), pos) * (HEAD_DIM ** -0.5)
    ka = rope(rms_norm(heads(ka, N_ATTN_HEADS), attn_k_norm), pos)
    o_attn = dilated_attention(qa, ka, heads(va, N_ATTN_HEADS))
    o_attn = o_attn.transpose(0, 2, 1, 3).reshape(B, S, ATTN_W).astype(h.dtype)

    log_a = jax.nn.log_sigmoid((glr @ gla_w_gate2 + gla_b_gate2).astype(jnp.float32)) / GLA_TAU
    o_gla = gla_chunked(heads(qg, N_GLA_HEADS).astype(jnp.float32) * (GLA_DK ** -0.5),
                        heads(kg, N_GLA_HEADS).astype(jnp.float32),
                        heads(vg, N_GLA_HEADS).astype(jnp.float32),
                        heads(log_a, N_GLA_HEADS))
    o_gla = rms_norm(o_gla, gla_out_norm).transpose(0, 2, 1, 3).reshape(B, S, GLA_VW).astype(h.dtype)
    o_gla = o_gla * jax.nn.silu(rg)

    return jnp.concatenate([o_attn, o_gla], axis=-1) @ w_out


def cross_attention(h, m, w_q, w_k, w_v, q_norm, k_norm, w_o):
    B, S, _ = h.shape
    M = m.shape[1]
    q = rms_norm((h @ w_q).reshape(B, S, N_XATTN_HEADS, HEAD_DIM).transpose(0, 2, 1, 3), q_norm)
    k = rms_norm((m @ w_k).reshape(B, M, N_XATTN_HEADS, HEAD_DIM).transpose(0, 2, 1, 3), k_norm)
    v = (m @ w_v).reshape(B, M, N_XATTN_HEADS, HEAD_DIM).transpose(0, 2, 1, 3)
    s = jnp.einsum("bhqd,bhkd->bhqk", q, k, preferred_element_type=jnp.float32) * (HEAD_DIM ** -0.5)
    p = jax.nn.softmax(s, axis=-1)
    o = jnp.einsum("bhqk,bhkd->bhqd", p, v.astype(jnp.float32)).astype(h.dtype)
    return o.transpose(0, 2, 1, 3).reshape(B, S, XATTN_W) @ w_o


def setup_inputs(seed: int = 0) -> dict:
    key = jax.random.key(seed)
    ks = iter(jax.random.split(key, 32))
    L = DEPTH

    def w(shape, fan_in):
        return jax.random.normal(next(ks), shape, jnp.float32) * (fan_in ** -0.5)

    def gain(shape):
        return 1.0 + 0.02 * jax.random.normal(next(ks), shape, jnp.float32)

    return {
        "x": jax.random.normal(next(ks), (BATCH, SEQ, D_MODEL), jnp.float32),
        "mem": jax.random.normal(next(ks), (BATCH, N_MEM, D_MODEL), jnp.float32),
        "ffn1_norm": gain((L, D_MODEL)),
        "ffn1_w_gate": w((L, D_MODEL, D_FF), D_MODEL),
        "ffn1_w_up": w((L, D_MODEL, D_FF), D_MODEL),
        "ffn1_w_down": w((L, D_FF, D_MODEL), D_FF),
        "mix_norm": gain((L, D_MODEL)),
        "w_in": w((L, D_MODEL, MIX_COLS), D_MODEL),
        "attn_q_norm": gain((L, HEAD_DIM)),
        "attn_k_norm": gain((L, HEAD_DIM)),
        "gla_w_gate2": w((L, GLA_GATE_RANK, GLA_KW), GLA_GATE_RANK),
        "gla_b_gate2": 0.01 * jax.random.normal(next(ks), (L, GLA_KW), jnp.float32),
        "gla_out_norm": gain((L, GLA_DV)),
        "w_out": w((L, ATTN_W + GLA_VW, D_MODEL), ATTN_W + GLA_VW),
        "xattn_norm": gain((L, D_MODEL)),
        "mem_norm": gain((L, D_MODEL)),
        "xattn_w_q": w((L, D_MODEL, XATTN_W), D_MODEL),
        "xattn_w_k": w((L, D_MODEL, XATTN_W), D_MODEL),
        "xattn_w_v": w((L, D_MODEL, XATTN_W), D_MODEL),
        "xattn_q_norm": gain((L, HEAD_DIM)),
        "xattn_k_norm": gain((L, HEAD_DIM)),
        "xattn_w_o": w((L, XATTN_W, D_MODEL), XATTN_W),
        "ffn2_norm": gain((L, D_MODEL)),
        "ffn2_w_gate": w((L, D_MODEL, D_FF), D_MODEL),
        "ffn2_w_up": w((L, D_MODEL, D_FF), D_MODEL),
        "ffn2_w_down": w((L, D_FF, D_MODEL), D_FF),
    }


def reference(x, mem, ffn1_norm, ffn1_w_gate, ffn1_w_up, ffn1_w_down, mix_norm, w_in,
              attn_q_norm, attn_k_norm, gla_w_gate2, gla_b_gate2, gla_out_norm, w_out,
              xattn_norm, mem_norm, xattn_w_q, xattn_w_k, xattn_w_v, xattn_q_norm, xattn_k_norm,
              xattn_w_o, ffn2_norm, ffn2_w_gate, ffn2_w_up, ffn2_w_down):
    S = x.shape[1]
    pos = jnp.arange(S, dtype=jnp.int32)
    for l in range(DEPTH):
        x = x + 0.5 * swiglu(rms_norm(x, ffn1_norm[l]), ffn1_w_gate[l], ffn1_w_up[l], ffn1_w_down[l])
        x = x + hybrid_mixer(rms_norm(x, mix_norm[l]), pos, w_in[l], attn_q_norm[l], attn_k_norm[l],
                             gla_w_gate2[l], gla_b_gate2[l], gla_out_norm[l], w_out[l])
        x = x + cross_attention(rms_norm(x, xattn_norm[l]), rms_norm(mem, mem_norm[l]),
                                xattn_w_q[l], xattn_w_k[l], xattn_w_v[l],
                                xattn_q_norm[l], xattn_k_norm[l], xattn_w_o[l])
        x = x + 0.5 * swiglu(rms_norm(x, ffn2_norm[l]), ffn2_w_gate[l], ffn2_w_up[l], ffn2_w_down[l])
    return x
```

```python
import numpy as np
from contextlib import ExitStack
import concourse.bass as bass
import concourse.mybir as mybir
from concourse.bass_utils import run_bass_kernel_spmd

F32 = mybir.dt.float32
BF16 = mybir.dt.bfloat16
AF = mybir.ActivationFunctionType
ALU = mybir.AluOpType

D = 2048
KC = 16
NT = 1024
TT = 512
DFF = 5632
EPS = 1e-6
ENGS = ("pe", "act", "dve", "pool", "sp")


class Buf:
    __slots__ = ("name", "t", "w_ev", "r_evs", "dsem", "dcount")

    def __init__(self, name, t):
        self.name = name
        self.t = t
        self.w_ev = None
        self.r_evs = []
        self.dsem = None
        self.dcount = 0

    def __getitem__(self, idx):
        return self.t[idx]


class Sched:
    def __init__(self, nc, stack):
        self.nc = nc
        self.stack = stack
        self.streams = {e: [] for e in ENGS}
        self.count = {e: 0 for e in ENGS}
        self.seen = {e: {} for e in ENGS}
        self.sems = {}
        for e in ENGS:
            self.sems[e] = stack.enter_context(nc.semaphore("s_" + e))
        self.nd = 0
        self.pending_sig = {e: False for e in ENGS}
        self.latest = {}

    def sb(self, name, shape, dtype, stack=None):
        t = (stack or self.stack).enter_context(self.nc.sbuf_tensor("sb_" + name, list(shape), dtype))
        return Buf(name, t)

    def ps(self, name, shape, dtype=F32):
        t = self.stack.enter_context(self.nc.psum_tensor("pp_" + name, list(shape), dtype))
        return Buf(name, t)

    def dram(self, name, ap):
        return Buf(name, ap)

    def _dsem(self, b):
        if b.dsem is None:
            self.nd += 1
            key = "d%d" % self.nd
            self.sems[key] = self.stack.enter_context(self.nc.semaphore(key))
            b.dsem = key
        return b.dsem

    def _need(self, eng, ev):
        if ev is None:
            return
        key, val = ev
        if key == eng and (eng == "pe" or val > self.count[eng]):
            return
        if self.seen[eng].get(key, 0) >= val:
            return
        self.seen[eng][key] = val
        self.streams[eng].append(("wait", key, val))

    def _deps(self, eng, reads, writes):
        for b in reads:
            self._need(eng, b.w_ev)
        for b in writes:
            self._need(eng, b.w_ev)
            for ev in b.r_evs:
                self._need(eng, ev)

    def _record(self, ev, reads, writes):
        self.latest[ev[0]] = max(self.latest.get(ev[0], 0), ev[1])
        for b in reads:
            if not b.r_evs or b.r_evs[-1] != ev:
                b.r_evs.append(ev)
        for b in writes:
            b.w_ev = ev
            b.r_evs = []

    def op(self, eng, fn, reads=(), writes=(), sig=True):
        self._deps(eng, reads, writes)
        if sig:
            self.count[eng] += 1
            ev = (eng, self.count[eng])
            self.streams[eng].append(("op", fn, eng, 1))
            self.pending_sig[eng] = False
        else:
            ev = (eng, self.count[eng] + 1)
            self.streams[eng].append(("op", fn, None, 0))
            self.pending_sig[eng] = True
        self._record(ev, reads, writes)
        return ev

    def dmaop(self, q, fn, reads=(), writes=()):
        assert len(writes) == 1
        self._deps(q, reads, writes)
        wb = writes[0]
        key = self._dsem(wb)
        wb.dcount += 16
        ev = (key, wb.dcount)
        self.streams[q].append(("op", fn, key, 16))
        self._record(ev, reads, writes)
        return ev

    def dma(self, q, out_ap, in_ap, reads=(), writes=()):
        return self.dmaop(q, lambda e: e.dma_start(out=out_ap, in_=in_ap), reads, writes)

    def barrier(self):
        for e in ENGS:
            assert not self.pending_sig[e]
        for e in ENGS:
            for key, val in self.latest.items():
                self._need(e, (key, val))

    def emit(self):
        nc = self.nc
        for e in ENGS:
            assert not self.pending_sig[e], "engine %s ends with unsignalled op" % e
        streams = self.streams
        self.streams = {e: [] for e in ENGS}
        if DBG.get('sim'):
            vals = self.__dict__.setdefault('simvals', {})
            pc = {e: 0 for e in ENGS}
            prog = True
            while prog:
                prog = False
                for e in ENGS:
                    st_ = streams[e]
                    while pc[e] < len(st_):
                        it = st_[pc[e]]
                        if it[0] == "wait":
                            if vals.get(it[1], 0) >= it[2]:
                                pc[e] += 1
                                prog = True
                            else:
                                break
                        else:
                            if it[2] is not None:
                                vals[it[2]] = vals.get(it[2], 0) + it[3]
                            pc[e] += 1
                            prog = True
            for e in ENGS:
                if pc[e] < len(streams[e]):
                    it = streams[e][pc[e]]
                    print("SIM DEADLOCK: engine", e, "stuck at", pc[e], "/", len(streams[e]), it[:3], "have", vals.get(it[1], 0))
            print("SIM block done", {e: len(streams[e]) for e in ENGS})
            return
        with nc.Block() as block:
            def run(engname):
                def body(eng):
                    for item in streams[engname]:
                        if item[0] == "wait":
                            eng.wait_ge(self.sems[item[1]], item[2])
                        else:
                            _, fn, key, inc = item
                            ins = fn(eng)
                            if key is not None:
                                ins.then_inc(self.sems[key], inc)
                return body
            block.tensor(run("pe"))
            block.scalar(run("act"))
            block.vector(run("dve"))
            block.gpsimd(run("pool"))
            block.sync(run("sp"))


DBG = {}


class K:
    pass


def MM(S, out_ap, lhsT, rhs, start, stop, reads, writes, sig=None):
    if sig is None:
        sig = stop
    S.op("pe", lambda e: e.matmul(out_ap, lhsT=lhsT, rhs=rhs, start=start, stop=stop), reads, writes, sig)


def ACT(S, out_ap, in_ap, func, reads, writes, bias=None, scale=None):
    kw = {}
    if bias is not None:
        kw["bias"] = bias
    if scale is not None:
        kw["scale"] = scale
    S.op("act", lambda e: e.activation(out=out_ap, in_=in_ap, func=func, **kw), reads, writes)


def TT_(S, out_ap, in0, in1, op, reads, writes, eng="dve"):
    S.op(eng, lambda e: e.tensor_tensor(out=out_ap, in0=in0, in1=in1, op=op), reads, writes)


def TS(S, out_ap, in0, s1, s2, op0, op1, reads, writes, eng="dve"):
    if s2 is None:
        S.op(eng, lambda e: e.tensor_scalar(out=out_ap, in0=in0, scalar1=s1, scalar2=None, op0=op0), reads, writes)
    else:
        S.op(eng, lambda e: e.tensor_scalar(out=out_ap, in0=in0, scalar1=s1, scalar2=s2, op0=op0, op1=op1), reads, writes)


def STT(S, out_ap, in0, scalar, in1, op0, op1, reads, writes, eng="dve"):
    S.op(eng, lambda e: e.scalar_tensor_tensor(out=out_ap, in0=in0, scalar=scalar, in1=in1, op0=op0, op1=op1), reads, writes)


def CP(S, out_ap, in_ap, reads, writes, eng="dve"):
    if eng == "act":
        S.op("act", lambda e: e.copy(out=out_ap, in_=in_ap), reads, writes)
    else:
        S.op(eng, lambda e: e.tensor_copy(out=out_ap, in_=in_ap), reads, writes)


class Rot:
    def __init__(self, items):
        self.items = items
        self.i = 0

    def next(self):
        b = self.items[self.i % len(self.items)]
        self.i += 1
        return b


def wload(k, view_fn, dram_ap):
    slot = k.wring.next()
    k.S.dma("pool", view_fn(slot), dram_ap, writes=[slot])
    return slot


def v3(n1, n2):
    return lambda slot: slot[:, 0:n1 * n2].rearrange("p (a b) -> p a b", a=n1)


def rstd_from_ps(k, ps, n, width, rs):
    S = k.S
    ACT(S, rs[:, 0:width], ps[:, 0:width], AF.Ln, [ps, k.epscol], [rs], bias=k.epscol[:, 0:1], scale=1.0 / n)
    ACT(S, rs[:, 0:width], rs[:, 0:width], AF.Exp, [rs], [rs], scale=-0.5)


def recip_act(k, out_ap, in_ap, reads, writes):
    S = k.S
    ACT(S, out_ap, in_ap, AF.Ln, reads, writes)
    ACT(S, out_ap, out_ap, AF.Exp, writes, writes, scale=-1.0)


def rmsnorm_fm(k, x_ap_fn, xbufs, gcol, out_ap_fn, outbufs, ntok):
    S = k.S
    for lo in range(0, ntok, TT):
        hi = min(lo + TT, ntok)
        w = hi - lo
        ps = k.psum.next()
        for kc in range(KC):
            sq = k.sqring.next()
            ACT(S, sq[:, 0:w], x_ap_fn(kc, lo, hi), AF.Square, xbufs(kc, lo), [sq])
            MM(S, ps[:, 0:w], k.ones[:, :], sq[:, 0:w], kc == 0, kc == KC - 1, [sq, k.ones], [ps], sig=True)
        rs = k.rsring.next()
        rstd_from_ps(k, ps, float(D), w, rs)
        for kc in range(KC):
            STT(S, out_ap_fn(kc, lo, hi), x_ap_fn(kc, lo, hi), k.gains[:, gcol + kc:gcol + kc + 1], rs[:, 0:w],
                ALU.mult, ALU.mult, xbufs(kc, lo) + [rs, k.gains], outbufs(kc, lo))


def ffn(k, wg, wu, wd):
    S = k.S
    xT, hA = k.xT, k.hA
    nblk = DBG.get('nblk', DFF // 512)
    for blk in range(nblk):
        c0 = blk * 512
        gs = [wload(k, v3(16, 256), wg[:, c0 + i * 256:c0 + (i + 1) * 256].rearrange("(kc p) c -> p kc c", p=128)) for i in range(2)]
        us = [wload(k, v3(16, 256), wu[:, c0 + i * 256:c0 + (i + 1) * 256].rearrange("(kc p) c -> p kc c", p=128)) for i in range(2)]
        aT = k.aT.next()
        for mc in range(4):
            gsl = v3(16, 256)(gs[mc // 2])
            usl = v3(16, 256)(us[mc // 2])
            off = (mc % 2) * 128
            psg = [k.psum.next() for _ in range(2)]
            for kc in range(KC):
                for tt in range(2):
                    MM(S, psg[tt][:, :], gsl[:, kc, off:off + 128], hA[:, kc, tt * TT:(tt + 1) * TT], kc == 0, kc == KC - 1,
                       [gs[mc // 2], hA], [psg[tt]])
            psu = [k.psum.next() for _ in range(2)]
            for kc in range(KC):
                for tt in range(2):
                    MM(S, psu[tt][:, :], usl[:, kc, off:off + 128], hA[:, kc, tt * TT:(tt + 1) * TT], kc == 0, kc == KC - 1,
                       [us[mc // 2], hA], [psu[tt]])
            for tt in range(2):
                sg = k.sgring.next()
                ACT(S, sg[:, :], psg[tt][:, :], AF.Silu, [psg[tt]], [sg])
                TT_(S, aT[:, mc, tt * TT:(tt + 1) * TT], sg[:, :], psu[tt][:, :], ALU.mult, [sg, psu[tt]], [aT])
        ds = [wload(k, v3(4, 1024), wd[c0:c0 + 512, i * 1024:(i + 1) * 1024].rearrange("(kc p) c -> p kc c", p=128)) for i in range(2)]
        for m in range(KC):
            dsl = v3(4, 1024)(ds[m // 8])
            off = (m % 8) * 128
            pss = [k.psum.next() for _ in range(2)]
            for kc in range(4):
                for tt in range(2):
                    MM(S, pss[tt][:, :], dsl[:, kc, off:off + 128], aT[:, kc, tt * TT:(tt + 1) * TT], kc == 0, kc == 3,
                       [ds[m // 8], aT], [pss[tt]])
            for tt in range(2):
                xb = xT[m][tt]
                STT(S, xb[:, :], pss[tt][:, :], 0.5, xb[:, :], ALU.mult, ALU.add, [pss[tt], xb], [xb])


def x_fn(k):
    return (lambda kc, lo, hi: k.xT[kc][lo // TT][:, 0:hi - lo]), (lambda kc, lo: [k.xT[kc][lo // TT]])


def h_fn(hbuf):
    return (lambda kc, lo, hi: hbuf[:, kc, lo:hi]), (lambda kc, lo: [hbuf])


def load_x(k, x_dram):
    S = k.S
    xv = x_dram.rearrange("(kc p) n -> p kc n", p=128)
    for kc in range(KC):
        for tt in range(2):
            S.dma("sp", k.xT[kc][tt][:, :], xv[:, kc, tt * TT:(tt + 1) * TT], writes=[k.xT[kc][tt]])


def headnorm(k, src, width, gcol, dst_f32):
    S = k.S
    for lo in range(0, width, TT):
        w = min(TT, width - lo)
        sq = k.sqring.next()
        ACT(S, sq[:, 0:w], src[:, lo:lo + w], AF.Square, [src], [sq])
        ps = k.psum.next()
        MM(S, ps[:, 0:w], k.ones[:, :], sq[:, 0:w], True, True, [sq, k.ones], [ps])
        rs = k.rsring.next()
        rstd_from_ps(k, ps, 128.0, w, rs)
        STT(S, dst_f32[:, lo:lo + w], src[:, lo:lo + w], k.gains[:, gcol:gcol + 1], rs[:, 0:w], ALU.mult, ALU.mult,
            [src, rs, k.gains], [dst_f32])


def rope(k, xn, width, tok0, dst_bf):
    S = k.S
    for lo in range(0, width, TT):
        w = min(TT, width - lo)
        ps = k.psum.next()
        MM(S, ps[:, 0:w], k.perm, xn[:, lo:lo + w], True, True, [xn, k.mats], [ps])
        t1 = k.rsring.next()
        TT_(S, t1[:, 0:w], xn[:, lo:lo + w], k.cos2[:, tok0 + lo:tok0 + lo + w], ALU.mult, [xn, k.cos2], [t1])
        t2 = k.rsring.next()
        TT_(S, t2[:, 0:w], ps[:, 0:w], k.sin2[:, tok0 + lo:tok0 + lo + w], ALU.mult, [ps, k.sin2], [t2])
        TT_(S, dst_bf[:, lo:lo + w], t1[:, 0:w], t2[:, 0:w], ALU.add, [t1, t2], [dst_bf])


def proj_fm(k, wslot, wsl, col0, srcs, dst_f32):
    S = k.S
    o = 0
    for hbuf, ntok in srcs:
        for lo in range(0, ntok, TT):
            ps = k.psum.next()
            for kc in range(KC):
                MM(S, ps[:, :], wsl[:, kc, col0:col0 + 128], hbuf[:, kc, lo:lo + TT], kc == 0, kc == KC - 1, [wslot, hbuf], [ps])
            CP(S, dst_f32[:, o:o + TT], ps[:, :], [ps], [dst_f32], eng="act")
            o += TT


def qk_pipeline(k, tiles, fillers=None):
    S = k.S
    st = [dict() for _ in tiles]

    def stageA(i):
        wslot, wsl, col0, hbuf, lo, gcol, tok0, dst_bf, dst_lo = tiles[i]
        ps = k.psum.next()
        for kc in range(KC):
            MM(S, ps[:, :], wsl[:, kc, col0:col0 + 128], hbuf[:, kc, lo:lo + TT], kc == 0, kc == KC - 1, [wslot, hbuf], [ps])
        xf = k.rsring.next()
        CP(S, xf[:, :], ps[:, :], [ps], [xf], eng="act")
        sq = k.sqring.next()
        ACT(S, sq[:, :], xf[:, :], AF.Square, [xf], [sq])
        st[i]["xf"], st[i]["sq"] = xf, sq

    def stageB(i):
        wslot, wsl, col0, hbuf, lo, gcol, tok0, dst_bf, dst_lo = tiles[i]
        xf, sq = st[i]["xf"], st[i]["sq"]
        ps2 = k.psum.next()
        MM(S, ps2[:, :], k.ones[:, :], sq[:, :], True, True, [sq, k.ones], [ps2])
        rs = k.rsring.next()
        rstd_from_ps(k, ps2, 128.0, TT, rs)
        xn = k.rsring.next()
        STT(S, xn[:, :], xf[:, :], k.gains[:, gcol:gcol + 1], rs[:, :], ALU.mult, ALU.mult, [xf, rs, k.gains], [xn])
        xnb = k.sqring.next()
        CP(S, xnb[:, :], xn[:, :], [xn], [xnb], eng="act")
        st[i]["xn"], st[i]["xnb"] = xn, xnb

    def stageC(i):
        wslot, wsl, col0, hbuf, lo, gcol, tok0, dst_bf, dst_lo = tiles[i]
        xn, xnb = st[i]["xn"], st[i]["xnb"]
        ps3 = k.psum.next()
        MM(S, ps3[:, :], k.permb[:, :], xnb[:, :], True, True, [xnb, k.permb], [ps3])
        t1 = k.rsring.next()
        TT_(S, t1[:, :], xn[:, :], k.cos2[:, tok0:tok0 + TT], ALU.mult, [xn, k.cos2], [t1], eng="pool")
        t2 = k.rsring.next()
        TT_(S, t2[:, :], ps3[:, :], k.sin2[:, tok0:tok0 + TT], ALU.mult, [ps3, k.sin2], [t2])
        TT_(S, dst_bf[:, dst_lo:dst_lo + TT], t1[:, :], t2[:, :], ALU.add, [t1, t2], [dst_bf], eng="pool")

    n = len(tiles)
    fillers = list(fillers or [])
    for step in range(n + 2):
        if step < n:
            stageA(step)
        elif fillers:
            fillers.pop(0)()
        if 0 <= step - 1 < n:
            stageB(step - 1)
        if 0 <= step - 2 < n:
            stageC(step - 2)
    while fillers:
        fillers.pop(0)()


def attn_wload(k, h, w_in):
    f = v3(16, 128)
    wv = wload(k, f, w_in[:, 2048 + h * 128:2048 + (h + 1) * 128].rearrange("(kc p) c -> p kc c", p=128))
    wk = wload(k, f, w_in[:, 1024 + h * 128:1024 + (h + 1) * 128].rearrange("(kc p) c -> p kc c", p=128))
    wq = wload(k, f, w_in[:, h * 128:(h + 1) * 128].rearrange("(kc p) c -> p kc c", p=128))
    return wq, wk, wv


def attn_head(k, h, w_in, st):
    S = k.S
    hC, hH = k.hC, k.hA
    if getattr(k, "attn_w", None) is None:
        k.attn_w = attn_wload(k, h, w_in)
    wq, wk, wv = k.attn_w
    k.attn_w = None
    wqs, wks, wvs = v3(16, 128)(wq), v3(16, 128)(wk), v3(16, 128)(wv)
    kT, qT = k.kT, k.qT
    tiles = []
    for i, (hb, lo) in enumerate(((hC, 0), (hC, TT), (hH, 0), (hH, TT))):
        tiles.append((wk, wks, 0, hb, lo, k.G_AK, i * TT, kT, i * TT))
    for i in range(2):
        tiles.append((wq, wqs, 0, hH, i * TT, k.G_AQ, 1024 + i * TT, qT, i * TT))
    def vgroup(hb, lo, o):
        def f():
            ps = k.psum.next()
            for kc in range(KC):
                MM(S, ps[:, :], wvs[:, kc, 0:128], hb[:, kc, lo:lo + TT], kc == 0, kc == KC - 1, [wv, hb], [ps])
            CP(S, k.vT[:, o:o + TT], ps[:, :], [ps], [k.vT], eng="act")
        return f
    vfill = [vgroup(hC, 0, 0), vgroup(hC, TT, TT), vgroup(hH, 0, 2 * TT), vgroup(hH, TT, 3 * TT)]
    vfill.pop(0)()
    vfill.pop(0)()
    qk_pipeline(k, tiles, vfill)
    if h < 7:
        k.attn_w = attn_wload(k, h + 1, w_in)

    V1, V4, V16, vT = k.V1, k.V4, k.V16, k.vT
    tjobs = []
    for j in range(7, 16):
        tjobs.append((V1, j - 7, 128 * j, 1))
    for c in range(4):
        for j in range(1, 4):
            tjobs.append((V4, c * 3 + j - 1, c + 4 * 128 * j, 4))
    for c in range(16):
        tjobs.append((V16, c, c, 16))
    for g0 in range(0, len(tjobs), 4):
        grp = tjobs[g0:g0 + 4]
        ps = k.psum.next()
        for i, (db, di, t0, stp) in enumerate(grp):
            S.op("pe", lambda e, ps=ps, i=i, t0=t0, stp=stp: e.transpose(ps[:, i * 128:(i + 1) * 128],
                                                                          vT[:, t0:t0 + stp * 127 + 1:stp], k.ident),
                 [vT, k.mats], [ps])
        db0, di0 = grp[0][0], grp[0][1]
        same = all(g[0] is db0 for g in grp) and all(grp[i][1] == di0 + i for i in range(len(grp)))
        if same:
            n = len(grp)
            CP(S, db0[:, di0:di0 + n, :], ps[:, 0:n * 128].rearrange("p (a b) -> p a b", a=n), [ps], [db0], eng="act")
        else:
            for i, (db, di, t0, stp) in enumerate(grp):
                CP(S, db[:, di, :], ps[:, i * 128:(i + 1) * 128], [ps], [db], eng="act")

    OD = k.OD
    sc = 128.0 ** -0.5
    M_same, M_next, M_nextC, M16 = k.M_same, k.M_next, k.M_nextC, k.M16

    def keys_ap(r, c, j, n=128, m_lo=0):
        start = c + r * (128 * j + m_lo)
        return kT[:, start:start + r * (n - 1) + 1:r]

    def q_ap(r, c, i, n=128, m_lo=0):
        start = c + r * (128 * i + m_lo) - 1024
        return qT[:, start:start + r * (n - 1) + 1:r]

    def acc_ap(r, c, i, n=128, m_lo=0):
        start = c + r * (128 * i + m_lo) - 1024
        return OD[:, :, start:start + r * (n - 1) + 1:r]

    sjobs = []
    pjobs = []
    s0 = len(sjobs)
    sjobs.append((1, 0, 7, [(8, M_nextC, 128, 0)]))
    for j in range(8, 16):
        parts = [(j, M_same, 128, 0)]
        if j < 15:
            parts.append((j + 1, M_next, 128, 0))
        sjobs.append((1, 0, j, parts))
        pjobs.append((1, 0, j, [(V1, V1[:, j - 1 - 7, :], s0 + j - 8, 0 if j == 8 else 128), (V1, V1[:, j - 7, :], s0 + j - 7, 0)], 128, 0, True))
    for c in range(4):
        s0 = len(sjobs)
        sjobs.append((4, c, 1, [(2, M_nextC, 128, 0)]))
        sjobs.append((4, c, 2, [(2, M_same, 128, 0), (3, M_next, 128, 0)]))
        sjobs.append((4, c, 3, [(3, M_same, 128, 0)]))
        pjobs.append((4, c, 2, [(V4, V4[:, c * 3 + 0, :], s0, 0), (V4, V4[:, c * 3 + 1, :], s0 + 1, 0)], 128, 0, False))
        pjobs.append((4, c, 3, [(V4, V4[:, c * 3 + 1, :], s0 + 1, 128), (V4, V4[:, c * 3 + 2, :], s0 + 2, 0)], 128, 0, False))
    for c in range(16):
        s0 = len(sjobs)
        sjobs.append((16, c, 0, [(0, M16, 64, 64)]))
        pjobs.append((16, c, 0, [(V16, V16[:, c, :], s0, 0)], 64, 64, False))

    Ps = {}
    state = {"next": 0}

    def emit_scores(upto):
        while state["next"] <= min(upto, len(sjobs) - 1):
            r, c, j, qparts = sjobs[state["next"]]
            ps = k.psum.next()
            P = k.Pring.next()
            o = 0
            for (i, mask, nq, m_lo) in qparts:
                MM(S, ps[:, o:o + nq], keys_ap(r, c, j), q_ap(r, c, i, nq, m_lo), True, True, [kT, qT], [ps])
                o += nq
            E = k.Ering.next()
            ACT(S, E[:, 0:o], ps[:, 0:o], AF.Exp, [ps], [E], scale=sc)
            o = 0
            for (i, mask, nq, m_lo) in qparts:
                TT_(S, P[:, o:o + nq], E[:, o:o + nq], mask, ALU.mult, [E, k.masks], [P], eng="pool")
                o += nq
            Ps[state["next"]] = P
            state["next"] += 1

    LOOK = 4
    for (r, c, i, contribs, nq, m_lo, first) in pjobs:
        emit_scores(max(ci[2] for ci in contribs) + LOOK)
        ps = k.psum.next()
        n = len(contribs)
        for idx, (vb, vap, sidx, po) in enumerate(contribs):
            P = Ps[sidx]
            MM(S, ps[:, 0:nq], vap, P[:, po:po + nq], idx == 0, idx == n - 1, [vb, P], [ps], sig=False)
        for idx, (vb, vap, sidx, po) in enumerate(contribs):
            P = Ps[sidx]
            MM(S, ps[:, 128:128 + nq], k.ones[:, :], P[:, po:po + nq], idx == 0, idx == n - 1, [k.ones, P], [ps], sig=(idx == n - 1))
        psv = ps[:, 0:256].rearrange("p (a b) -> p a b", a=2)[:, :, 0:nq]
        if first:
            CP(S, acc_ap(r, c, i, nq, m_lo), psv, [ps], [OD])
        else:
            TT_(S, acc_ap(r, c, i, nq, m_lo), acc_ap(r, c, i, nq, m_lo), psv, ALU.add, [ps, OD], [OD])
    recip_act(k, OD[:, 1, :], OD[:, 1, :], [OD], [OD])
    TT_(S, k.oT[:, h, :], OD[:, 0, :], OD[:, 1, :], ALU.mult, [OD], [k.oT])


def gla_prep(k, w_in):
    S = k.S
    wsl_fn = v3(16, 16)
    wg = wload(k, wsl_fn, w_in[:, 6144:6160].rearrange("(kc p) c -> p kc c", p=128))
    wsl = wsl_fn(wg)
    S.op("dve", lambda e: e.memset(k.glrT[:, :], 1.0), [], [k.glrT])
    o = 0
    for hb in (k.hC, k.hA):
        for lo in range(0, NT, TT):
            ps = k.psum.next()
            for kc in range(KC):
                MM(S, ps[0:16, :], wsl[:, kc, :], hb[:, kc, lo:lo + TT], kc == 0, kc == KC - 1, [wg, hb], [ps])
            CP(S, k.glrT[0:16, o:o + TT], ps[0:16, :], [ps], [k.glrT], eng="act")
            o += TT


def gla_head(k, g, w_in):
    S = k.S
    hC, hH = k.hC, k.hA
    QG0, KG0, VG0, RG0 = 3072, 3584, 4096, 5120
    f128 = v3(16, 128)
    f256 = v3(16, 256)
    wq = wload(k, f128, w_in[:, QG0 + g * 128:QG0 + (g + 1) * 128].rearrange("(kc p) c -> p kc c", p=128))
    wk = wload(k, f128, w_in[:, KG0 + g * 128:KG0 + (g + 1) * 128].rearrange("(kc p) c -> p kc c", p=128))
    wv = wload(k, f256, w_in[:, VG0 + g * 256:VG0 + (g + 1) * 256].rearrange("(kc p) c -> p kc c", p=128))
    kf, lbuf, cl = k.kf, k.lbuf, k.cl
    proj_fm(k, wk, f128(wk), 0, [(hC, NT), (hH, NT)], kf)
    for lo in range(0, 2048, TT):
        ps = k.psum.next()
        MM(S, ps[:, :], k.w2[0:17, g * 128:(g + 1) * 128], k.glrT[0:17, lo:lo + TT], True, True, [k.w2, k.glrT], [ps])
        e1 = k.rsring.next()
        ACT(S, e1[:, :], ps[:, :], AF.Exp, [ps], [e1], scale=-1.0)
        ACT(S, lbuf[:, lo:lo + TT], e1[:, :], AF.Ln, [e1], [lbuf], bias=k.onecol[:, 0:1])
        S.op("dve", lambda e, lo=lo: e.tensor_tensor_scan(out=cl[:, lo:lo + TT], data0=k.rmask[:, :], data1=lbuf[:, lo:lo + TT],
                                                           initial=0.0, op0=ALU.mult, op1=ALU.add), [k.rmask, lbuf], [cl])
    for i in range(2):
        lo = i * TT
        ps = k.psum.next()
        for kc in range(KC):
            MM(S, ps[:, :], f128(wq)[:, kc, :], hH[:, kc, lo:lo + TT], kc == 0, kc == KC - 1, [wq, hH], [ps])
        eb = k.rsring.next()
        ACT(S, eb[:, :], cl[:, 1024 + lo:1024 + lo + TT], AF.Exp, [cl], [eb], scale=-1.0 / 16)
        STT(S, k.qT[:, lo:lo + TT], ps[:, :], 128.0 ** -0.5, eb[:, :], ALU.mult, ALU.mult, [ps, eb], [k.qT])
        enb = k.rsring.next()
        ACT(S, enb[:, :], cl[:, 1024 + lo:1024 + lo + TT], AF.Exp, [cl], [enb], scale=1.0 / 16)
        TT_(S, k.keT[:, lo:lo + TT], kf[:, 1024 + lo:1024 + lo + TT], enb[:, :], ALU.mult, [kf, enb], [k.keT])
    dtot = k.dtot
    ACT(S, dtot[:, :], cl[:, 63:2048:64], AF.Exp, [cl], [dtot], scale=-1.0 / 16)
    dl = lbuf
    clv = cl[:, :].rearrange("p (n c) -> p n c", c=64)
    S.op("dve", lambda e: e.tensor_tensor(out=dl[:, :].rearrange("p (n c) -> p n c", c=64), in0=clv[:, :, 63:64].to_broadcast([128, 32, 64]),
                                          in1=clv, op=ALU.subtract), [cl], [dl])
    ACT(S, dl[:, :], dl[:, :], AF.Exp, [dl], [dl], scale=-1.0 / 16)
    TT_(S, kf[:, :], kf[:, :], dl[:, :], ALU.mult, [kf, dl], [kf])
    kdecT = kf
    Sfr, Sbr = k.Sfr, k.Sbr
    Sf = Sfr.next()
    S.op("dve", lambda e, Sf0=Sf: e.memset(Sf0[:, :], 0.0), [], [Sf])
    Sb = None
    og = k.og
    for blk in range(16):
        hb, lo = (hC, blk * 128) if blk < 8 else (hH, (blk - 8) * 128)
        ps = k.psum.next()
        for kc in range(KC):
            MM(S, ps[:, 0:256], hb[:, kc, lo:lo + 128], f256(wv)[:, kc, :], kc == 0, kc == KC - 1, [wv, hb], [ps])
        vt = k.vtring.next()
        CP(S, vt[:, :], ps[:, 0:256], [ps], [vt], eng="act")
        pst = k.psum.next()
        S.op("pe", lambda e, pst=pst, blk=blk: e.transpose(pst[:, 0:128], kdecT[:, blk * 128:(blk + 1) * 128], k.ident),
             [kdecT, k.mats], [pst])
        kd = k.kdring.next()
        CP(S, kd[:, :], pst[:, 0:128], [pst], [kd])
        own = blk >= 8
        Sb_before = [Sb, None]
        for ch in range(2):
            n = blk * 2 + ch
            p0 = ch * 64
            psd = k.psum.next()
            MM(S, psd[:, 0:256], kd[p0:p0 + 64, :], vt[p0:p0 + 64, :], True, True, [kd, vt], [psd])
            Sn = Sfr.next()
            STT(S, Sn[:, :], Sf[:, :], dtot[:, n:n + 1], psd[:, 0:256], ALU.mult, ALU.add, [Sf, dtot, psd], [Sn])
            Sf = Sn
            if n >= 15 and n < 31:
                Sb = Sbr.next()
                CP(S, Sb[:, :], Sf[:, :], [Sf], [Sb], eng="act")
            if ch == 0:
                Sb_before[1] = Sb
        if own:
            t0 = (blk - 8) * 128
            psa = k.psum.next()
            MM(S, psa[:, 0:128], k.keT[:, t0:t0 + 128], k.qT[:, t0:t0 + 128], True, True, [k.keT, k.qT], [psa])
            At = k.Pring.next()
            TT_(S, At[:, 0:128], psa[:, 0:128], k.MG, ALU.mult, [psa, k.masks], [At])
            pso = [k.psum.next() for _ in range(2)]
            for ch in range(2):
                p0 = ch * 64
                tq = (blk - 8) * 128 + ch * 64
                Sbb = Sb_before[ch]
                for dc in range(2):
                    MM(S, pso[dc][:, p0:p0 + 64], vt[:, dc * 128:(dc + 1) * 128], At[:, p0:p0 + 64], True, False, [vt, At], [pso[dc]], sig=False)
                    MM(S, pso[dc][:, p0:p0 + 64], Sbb[:, dc * 128:(dc + 1) * 128], k.qT[:, tq:tq + 64], False, True, [Sbb, k.qT], [pso[dc]],
                       sig=True)
            for dc in range(2):
                CP(S, og[:, dc, t0:t0 + 128], pso[dc][:, 0:128], [pso[dc]], [og], eng="act")
    wr = wload(k, f256, w_in[:, RG0 + g * 256:RG0 + (g + 1) * 256].rearrange("(kc p) c -> p kc c", p=128))
    for lo in range(0, NT, TT):
        ps = k.psum.next()
        for dc in range(2):
            sq = k.sqring.next()
            ACT(S, sq[:, :], og[:, dc, lo:lo + TT], AF.Square, [og], [sq])
            MM(S, ps[:, :], k.ones[:, :], sq[:, :], dc == 0, dc == 1, [sq, k.ones], [ps], sig=True)
        rs = k.rsring.next()
        rstd_from_ps(k, ps, 256.0, TT, rs)
        for dc in range(2):
            psr = k.psum.next()
            for kc in range(KC):
                MM(S, psr[:, :], f256(wr)[:, kc, dc * 128:(dc + 1) * 128], hH[:, kc, lo:lo + TT], kc == 0, kc == KC - 1, [wr, hH], [psr])
            sg = k.rsring.next()
            ACT(S, sg[:, :], psr[:, :], AF.Silu, [psr], [sg])
            t1 = k.rsring.next()
            STT(S, t1[:, :], og[:, dc, lo:lo + TT], k.gains[:, k.G_GO + dc:k.G_GO + dc + 1], rs[:, :], ALU.mult, ALU.mult,
                [og, rs, k.gains], [t1])
            TT_(S, k.oT[:, 8 + 2 * g + dc, lo:lo + TT], t1[:, :], sg[:, :], ALU.mult, [t1, sg], [k.oT])


def add_proj(k, w, nkc, src_bf):
    S = k.S
    for mp in range(8):
        fn = v3(nkc, 256)
        sl = wload(k, fn, w[:, mp * 256:(mp + 1) * 256].rearrange("(kc p) c -> p kc c", p=128))
        for mi in range(2):
            m = mp * 2 + mi
            pss = [k.psum.next() for _ in range(2)]
            for kc in range(nkc):
                for tt in range(2):
                    MM(S, pss[tt][:, :], fn(sl)[:, kc, mi * 128:(mi + 1) * 128], src_bf[:, kc, tt * TT:(tt + 1) * TT], kc == 0, kc == nkc - 1,
                       [sl, src_bf], [pss[tt]])
            for tt in range(2):
                xb = k.xT[m][tt]
                TT_(S, xb[:, :], xb[:, :], pss[tt][:, :], ALU.add, [pss[tt], xb], [xb])


def xattn_mem(k, wk, wv, memT_d):
    S = k.S
    memf, mT = k.memf, k.mT
    rmsnorm_fm(k, lambda kc, lo, hi: memf[:, kc, lo:hi], lambda kc, lo: [memf], k.G_MEM,
               lambda kc, lo, hi: mT[:, kc, lo:hi], lambda kc, lo: [mT], 256)
    f256 = v3(16, 256)
    kx = k.kx
    for hp in range(2):
        sl = wload(k, f256, wk[:, hp * 256:(hp + 1) * 256].rearrange("(kc p) c -> p kc c", p=128))
        for hi_ in range(2):
            hx = hp * 2 + hi_
            ps = k.psum.next()
            for kc in range(KC):
                MM(S, ps[:, 0:256], f256(sl)[:, kc, hi_ * 128:(hi_ + 1) * 128], mT[:, kc, :], kc == 0, kc == KC - 1, [sl, mT], [ps])
            CP(S, k.kf[:, 0:256], ps[:, 0:256], [ps], [k.kf], eng="act")
            headnorm(k, k.kf, 256, k.G_XK, k.kn)
            CP(S, kx[:, hx, :], k.kn[:, 0:256], [k.kn], [kx])
    vx = k.vx
    sls = [wload(k, f256, wv[:, i * 256:(i + 1) * 256].rearrange("(kc p) c -> p kc c", p=128)) for i in range(2)]
    for blk in range(2):
        ps = k.psum.next()
        for i in range(2):
            for kc in range(KC):
                MM(S, ps[:, i * 256:(i + 1) * 256], mT[:, kc, blk * 128:(blk + 1) * 128], f256(sls[i])[:, kc, :], kc == 0, kc == KC - 1,
                   [sls[i], mT], [ps], sig=(kc == KC - 1))
        CP(S, vx[:, blk, :], ps[:, :], [ps], [vx], eng="act")


def xattn(k, wq, wo):
    S = k.S
    kx, vx = k.kx, k.vx
    f256 = v3(16, 256)
    sc = 128.0 ** -0.5
    for hp in range(2):
        sl = wload(k, f256, wq[:, hp * 256:(hp + 1) * 256].rearrange("(kc p) c -> p kc c", p=128))
        for hi_ in range(2):
            hx = hp * 2 + hi_
            proj_fm(k, sl, f256(sl), hi_ * 128, [(k.hA, NT)], k.qf)
            headnorm(k, k.qf, 1024, k.G_XQ, k.qn)
            CP(S, k.qT[:, :], k.qn[:, :], [k.qn], [k.qT])
            for tt in range(2):
                Es = []
                for blk in range(2):
                    ps = k.psum.next()
                    MM(S, ps[:, :], kx[:, hx, blk * 128:(blk + 1) * 128], k.qT[:, tt * TT:(tt + 1) * TT], True, True, [kx, k.qT], [ps])
                    E = k.Ering.next()
                    ACT(S, E[:, :], ps[:, :], AF.Exp, [ps], [E], scale=sc)
                    Es.append(E)
                pso = k.psum.next()
                psd = k.psum.next()
                for blk in range(2):
                    MM(S, pso[:, :], vx[:, blk, hx * 128:(hx + 1) * 128], Es[blk][:, :], blk == 0, blk == 1, [vx, Es[blk]], [pso])
                for blk in range(2):
                    MM(S, psd[:, :], k.ones[:, :], Es[blk][:, :], blk == 0, blk == 1, [k.ones, Es[blk]], [psd])
                rd = k.rsring.next()
                recip_act(k, rd[:, :], psd[:, :], [psd], [rd])
                TT_(S, k.oT[:, hx, tt * TT:(tt + 1) * TT], pso[:, :], rd[:, :], ALU.mult, [pso, rd], [k.oT])
    add_proj(k, wo, 4, k.oT)


NG = 96
G_FFN1, G_MIX, G_XA, G_MEM, G_FFN2 = 0, 16, 32, 48, 64
G_AQ, G_AK, G_XQ, G_XK, G_GO, G_B2 = 80, 81, 82, 83, 84, 86
NMASK = 512


def wout_to_dram(k, w, src_bf, xin_d, xin_buf, xout_d, xout_buf):
    S = k.S
    xi = xin_d.rearrange("(kc p) n -> p kc n", p=128)
    xo = xout_d.rearrange("(kc p) n -> p kc n", p=128)
    fn = v3(16, 256)
    for mp in range(8):
        sl = wload(k, fn, w[:, mp * 256:(mp + 1) * 256].rearrange("(kc p) c -> p kc c", p=128))
        for mi in range(2):
            m = mp * 2 + mi
            pss = [k.psum.next() for _ in range(2)]
            for kc in range(16):
                for tt in range(2):
                    MM(S, pss[tt][:, :], fn(sl)[:, kc, mi * 128:(mi + 1) * 128], src_bf[:, kc, tt * TT:(tt + 1) * TT], kc == 0, kc == 15,
                       [sl, src_bf], [pss[tt]])
            for tt in range(2):
                xs = k.stage.next()
                S.dma("sp", xs[:, :], xi[:, m, tt * TT:(tt + 1) * TT], reads=[xin_buf], writes=[xs])
                TT_(S, xs[:, :], xs[:, :], pss[tt][:, :], ALU.add, [pss[tt], xs], [xs])
                S.dma("sp", xo[:, m, tt * TT:(tt + 1) * TT], xs[:, :], reads=[xs], writes=[xout_buf])


def build_program(stage=99, debug=False):
    nc = bass.Bass("TRN2", target_bir_lowering=False)

    def din(name, shape, dt=F32):
        return nc.dram_tensor(name, list(shape), dt, kind="ExternalInput").ap()

    xC_d = din("xC", [D, NT])
    xH_d = din("xH", [D, NT])
    memT_d = din("memT", [D, 256])
    W = {}
    for nm, shp in (("ffn1_w_gate", [D, DFF]), ("ffn1_w_up", [D, DFF]), ("ffn1_w_down", [DFF, D]), ("w_in", [D, 6160]),
                    ("w_out", [D, D]), ("xattn_w_q", [D, 512]), ("xattn_w_k", [D, 512]), ("xattn_w_v", [D, 512]),
                    ("xattn_w_o", [512, D]), ("ffn2_w_gate", [D, DFF]), ("ffn2_w_up", [D, DFF]), ("ffn2_w_down", [DFF, D])):
        W[nm] = din(nm, shp)
    gains_d = din("gains", [128, NG])
    w2_d = din("w2aug", [17, 512])
    cos_d = din("cos2", [128, 2048])
    sin_d = din("sin2", [128, 2048])
    masks_d = din("masks", [128, NMASK + 128])
    mats_d = din("mats", [128, 256])
    out_d = nc.dram_tensor("outT", [D, NT], F32, kind="ExternalOutput").ap()
    x1_d = nc.dram_tensor("x1_scratch", [D, NT], F32, kind="ExternalOutput").ap()
    x2_d = nc.dram_tensor("x2_scratch", [D, NT], F32, kind="ExternalOutput").ap()

    with ExitStack() as st:
        S = Sched(nc, st)
        k = K()
        k.S = S
        k.nc = nc
        k.G_AQ, k.G_AK, k.G_XQ, k.G_XK, k.G_GO, k.G_MEM = G_AQ, G_AK, G_XQ, G_XK, G_GO, G_MEM
        k.psum = Rot([S.ps("ps%d" % i, [128, 512]) for i in range(8)])
        k.gains = S.sb("gains", [128, NG], F32)
        k.ones = S.sb("ones", [128, 128], BF16)
        k.onecol = S.sb("onecol", [128, 1], F32)
        k.masks = S.sb("masks", [128, NMASK + 128], BF16)
        k.M_same = k.masks[:, 0:128]
        k.M_next = k.masks[:, 128:256]
        k.M_nextC = k.masks[:, 256:384]
        k.M16 = k.masks[:, 384:448]
        k.MG = k.masks[:, NMASK:NMASK + 128]
        k.mats = S.sb("mats", [128, 256], F32)
        k.ident = k.mats[:, 0:128]
        k.perm = k.mats[:, 128:256]
        k.w2 = S.sb("w2", [32, 512], BF16)
        k.hA = S.sb("hA", [128, KC, NT], BF16)
        k.wring = Rot([S.sb("wr%d" % i, [128, 4096], BF16) for i in range(4)])
        k.sqring = Rot([S.sb("sq%d" % i, [128, TT], BF16) for i in range(5)])
        k.rsring = Rot([S.sb("rs%d" % i, [128, TT], F32) for i in range(8)])
        x1b = S.dram("x1d", x1_d)
        x2b = S.dram("x2d", x2_d)
        outb = S.dram("out", out_d)

        S.dma("sp", k.gains[:, :], gains_d, writes=[k.gains])
        S.dma("sp", k.mats[:, :], mats_d, writes=[k.mats])
        S.dma("pool", k.masks[:, :], masks_d, writes=[k.masks])
        S.op("dve", lambda e: e.memset(k.w2[:, :], 0.0), [], [k.w2])
        S.dma("pool", k.w2[0:17, :], w2_d, writes=[k.w2])
        S.op("dve", lambda e: e.memset(k.ones[:, :], 1.0), [], [k.ones])
        k.permb = S.sb("permb", [128, 128], BF16)
        CP(S, k.permb[:, :], k.perm, [k.mats], [k.permb])
        S.op("dve", lambda e: e.memset(k.onecol[:, :], 1.0), [], [k.onecol])
        k.epscol = S.sb("epscol", [128, 1], F32)
        S.op("dve", lambda e: e.memset(k.epscol[:, :], EPS), [], [k.epscol])

        def alloc_x(ph, tag):
            k.xT = [[S.sb("x%s%d_%d" % (tag, kc, tt), [128, TT], F32, ph) for tt in range(2)] for kc in range(KC)]

        def alloc_ffn(ph, tag):
            k.aT = Rot([S.sb("aT%s%d" % (tag, i), [128, 4, NT], BF16, ph) for i in range(1)])
            k.sgring = Rot([S.sb("sg%s%d" % (tag, i), [128, TT], F32, ph) for i in range(2)])

        def store_x(dst, dstbuf):
            dv = dst.rearrange("(kc p) n -> p kc n", p=128)
            for kc in range(KC):
                for tt in range(2):
                    S.dma("sp", dv[:, kc, tt * TT:(tt + 1) * TT], k.xT[kc][tt][:, :], reads=[k.xT[kc][tt]], writes=[dstbuf])

        def finish(ph_note=""):
            S.barrier()
            S.emit()

        with ExitStack() as ph12:
            k.hC = S.sb("hC", [128, KC, NT], BF16, ph12)
            with ExitStack() as ph:
                alloc_x(ph, "a")
                alloc_ffn(ph, "a")
                xf, xb = x_fn(k)
                for which in range(2):
                    load_x(k, xC_d if which == 0 else xH_d)
                    if DBG.get('p1') == 'ls':
                        continue
                    rmsnorm_fm(k, xf, xb, G_FFN1, *h_fn(k.hA), NT)
                    if DBG.get('p1') == 'n1':
                        continue
                    ffn(k, W["ffn1_w_gate"], W["ffn1_w_up"], W["ffn1_w_down"])
                    tgt = k.hC if which == 0 else k.hA
                    rmsnorm_fm(k, xf, xb, G_MIX, *h_fn(tgt), NT)
                store_x(x1_d, x1b)
                if stage == 1:
                    store_x(out_d, outb)
                finish()
            if stage == 1:
                return nc

            with ExitStack() as ph:
                k.oT = S.sb("oT", [128, 16, NT], BF16, ph)
                k.qT = S.sb("qT", [128, 1024], BF16, ph)
                k.Pring = Rot([S.sb("P%d" % i, [128, 256], BF16, ph) for i in range(8)])
                with ExitStack() as pa:
                    k.cos2 = S.sb("cos2", [128, 2048], F32, pa)
                    k.sin2 = S.sb("sin2", [128, 2048], F32, pa)
                    S.dma("sp", k.cos2[:, :], cos_d, writes=[k.cos2])
                    S.dma("sp", k.sin2[:, :], sin_d, writes=[k.sin2])
                    k.kT = S.sb("kT", [128, 2048], BF16, pa)
                    k.V1 = S.sb("V1", [128, 9, 128], BF16, pa)
                    k.V4 = S.sb("V4", [128, 12, 128], BF16, pa)
                    k.V16 = S.sb("V16", [128, 16, 128], BF16, pa)
                    k.OD = S.sb("OD", [128, 2, 1024], F32, pa)
                    k.vT = S.sb("vT", [128, 2048], F32, pa)
                    k.Ering = Rot([S.sb("E%d" % i, [128, 256], BF16, pa) for i in range(4)])
                    for h in range(8):
                        attn_head(k, h, W["w_in"], pa)
                    finish()
                pg = ph
                k.glrT = S.sb("glrT", [32, 2048], BF16, pg)
                k.kf = S.sb("kf", [128, 2048], F32, pg)
                k.lbuf = S.sb("lbuf", [128, 2048], F32, pg)
                k.cl = S.sb("cl", [128, 2048], F32, pg)
                k.rmask = S.sb("rmask", [128, TT], F32, pg)
                k.keT = S.sb("keT", [128, 1024], BF16, pg)
                k.dtot = S.sb("dtot", [128, 32], F32, pg)
                k.og = S.sb("og", [128, 2, 1024], F32, pg)
                k.Sfr = Rot([S.sb("Sf%d" % i, [128, 256], F32, pg) for i in range(4)])
                k.Sbr = Rot([S.sb("Sb%d" % i, [128, 256], BF16, pg) for i in range(4)])
                k.vtring = Rot([S.sb("vt%d" % i, [128, 256], BF16, pg) for i in range(3)])
                k.kdring = Rot([S.sb("kd%d" % i, [128, 128], BF16, pg) for i in range(3)])
                k.stage = k.rsring
                S.op("dve", lambda e: e.memset(k.rmask[:, :], 1.0), [], [k.rmask])
                S.op("dve", lambda e: e.memset(k.rmask[:, 0:TT:64], 0.0), [k.rmask], [k.rmask])
                gla_prep(k, W["w_in"])
                for g in range(4):
                    gla_head(k, g, W["w_in"])
                if stage == 2:
                    ob = out_d.rearrange("(kc p) n -> p kc n", p=128)
                    for c in range(16):
                        xs = k.stage.next()
                        for tt in range(2):
                            CP(S, xs[:, :], k.oT[:, c, tt * TT:(tt + 1) * TT], [k.oT], [xs])
                            S.dma("sp", ob[:, c, tt * TT:(tt + 1) * TT], xs[:, :], reads=[xs], writes=[outb])
                    finish()
                    return nc
                wout_to_dram(k, W["w_out"], k.oT, x1_d, x1b, x2_d, x2b)
                finish()

        with ExitStack() as ph3:
            k.kx = S.sb("kx", [128, 4, 256], BF16, ph3)
            k.vx = S.sb("vx", [128, 2, 512], BF16, ph3)
            alloc_x(ph3, "b")
            xf, xb = x_fn(k)
            with ExitStack() as ph:
                k.memf = S.sb("memf", [128, KC, 256], F32, ph)
                k.mT = S.sb("mT", [128, KC, 256], BF16, ph)
                k.kf = S.sb("kfx", [128, 256], F32, ph)
                k.kn = S.sb("knx", [128, 256], F32, ph)
                k.qf = S.sb("qfx", [128, 1024], F32, ph)
                k.qn = S.sb("qnx", [128, 1024], F32, ph)
                k.qT = S.sb("qTx", [128, 1024], BF16, ph)
                k.Ering = Rot([S.sb("Ex%d" % i, [128, 512], BF16, ph) for i in range(4)])
                k.oT = S.sb("oTx", [128, 4, NT], BF16, ph)
                S.dma("sp", k.memf[:, :, :], memT_d.rearrange("(kc p) n -> p kc n", p=128), writes=[k.memf])
                load_x(k, x2_d)
                xattn_mem(k, W["xattn_w_k"], W["xattn_w_v"], memT_d)
                rmsnorm_fm(k, xf, xb, G_XA, *h_fn(k.hA), NT)
                xattn(k, W["xattn_w_q"], W["xattn_w_o"])
                if stage == 4:
                    store_x(out_d, outb)
                    finish()
                    return nc
                finish()
            with ExitStack() as ph:
                alloc_ffn(ph, "b")
                rmsnorm_fm(k, xf, xb, G_FFN2, *h_fn(k.hA), NT)
                ffn(k, W["ffn2_w_gate"], W["ffn2_w_up"], W["ffn2_w_down"])
                store_x(out_d, outb)
                finish()
    return nc


def _fm(a):
    return np.ascontiguousarray(np.asarray(a, dtype=np.float32).T)


def _col(v):
    v = np.asarray(v, dtype=np.float32).reshape(-1, 128)
    return np.ascontiguousarray(v.T)


def _consts(half):
    p = np.arange(128)
    inv = (10000.0 ** (-(np.arange(64, dtype=np.float32)) / 64.0)).astype(np.float32)
    pos = np.arange(2048, dtype=np.float32) - (0.0 if half == 1 else 1024.0)
    ang = pos[None, :].astype(np.float32) * inv[p % 64][:, None]
    cos2 = np.cos(ang).astype(np.float32)
    sin2 = np.sin(ang).astype(np.float32)
    sin2[:64] *= -1.0
    kk = p[:, None]
    qq = p[None, :]
    valid = 1.0 if half == 1 else 0.0
    m_same = (kk <= qq).astype(np.float32)
    m_next = (kk >= qq).astype(np.float32)
    m_nextc = m_next * valid
    q16 = 64 + np.arange(64)[None, :]
    m16 = np.where(kk < 64, valid, (kk <= q16).astype(np.float32)).astype(np.float32)
    m16 = np.broadcast_to(m16, (128, 64))
    pad = np.zeros((128, 64), np.float32)
    mg = ((kk // 64 == qq // 64) & (kk <= qq)).astype(np.float32)
    masks = np.concatenate([m_same, m_next, m_nextc, m16, pad, mg], axis=1).astype(np.float32)
    ident = np.eye(128, dtype=np.float32)
    perm = np.zeros((128, 128), np.float32)
    perm[(p + 64) % 128, p] = 1.0
    mats = np.concatenate([ident, perm], axis=1)
    return cos2, sin2, np.ascontiguousarray(masks), np.ascontiguousarray(mats)


_PROG = {}


def kernel(x, mem, ffn1_norm, ffn1_w_gate, ffn1_w_up, ffn1_w_down, mix_norm, w_in,
           attn_q_norm, attn_k_norm, gla_w_gate2, gla_b_gate2, gla_out_norm, w_out,
           xattn_norm, mem_norm, xattn_w_q, xattn_w_k, xattn_w_v, xattn_q_norm, xattn_k_norm,
           xattn_w_o, ffn2_norm, ffn2_w_gate, ffn2_w_up, ffn2_w_down, _stage=99, _cores=None):
    x = np.asarray(x, np.float32)
    mem = np.asarray(mem, np.float32)
    f = lambda a: np.ascontiguousarray(np.asarray(a, np.float32)[0])
    gains = np.zeros((128, NG), np.float32)
    gains[:, G_FFN1:G_FFN1 + 16] = _col(f(ffn1_norm))
    gains[:, G_MIX:G_MIX + 16] = _col(f(mix_norm))
    gains[:, G_XA:G_XA + 16] = _col(f(xattn_norm))
    gains[:, G_MEM:G_MEM + 16] = _col(f(mem_norm))
    gains[:, G_FFN2:G_FFN2 + 16] = _col(f(ffn2_norm))
    gains[:, G_AQ] = f(attn_q_norm)
    gains[:, G_AK] = f(attn_k_norm)
    gains[:, G_XQ] = f(xattn_q_norm)
    gains[:, G_XK] = f(xattn_k_norm)
    gains[:, G_GO:G_GO + 2] = _col(f(gla_out_norm))
    w2aug = np.ascontiguousarray(np.concatenate([f(gla_w_gate2), f(gla_b_gate2)[None, :]], axis=0))
    common = {
        "ffn1_w_gate": f(ffn1_w_gate), "ffn1_w_up": f(ffn1_w_up), "ffn1_w_down": f(ffn1_w_down), "w_in": f(w_in),
        "w_out": f(w_out), "xattn_w_q": f(xattn_w_q), "xattn_w_k": f(xattn_w_k), "xattn_w_v": f(xattn_w_v),
        "xattn_w_o": f(xattn_w_o), "ffn2_w_gate": f(ffn2_w_gate), "ffn2_w_up": f(ffn2_w_up), "ffn2_w_down": f(ffn2_w_down),
        "gains": gains, "w2aug": w2aug,
    }
    B = x.shape[0]
    jobs = [(b, s) for b in range(B) for s in range(2)]
    if _cores is not None:
        jobs = jobs[:_cores]
    consts = {s: _consts(s) for s in (0, 1)}
    in_maps = []
    for (b, s) in jobs:
        cos2, sin2, masks, mats = consts[s]
        xh = _fm(x[b, s * NT:(s + 1) * NT])
        xc = _fm(x[b, 0:NT]) if s == 1 else np.zeros((D, NT), np.float32)
        m = dict(common)
        m.update({"xC": xc, "xH": xh, "memT": _fm(mem[b]), "cos2": cos2, "sin2": sin2, "masks": masks, "mats": mats})
        in_maps.append(m)
    key = _stage
    if key not in _PROG:
        _PROG[key] = build_program(stage=_stage)
    res = run_bass_kernel_spmd(_PROG[key], in_maps, core_ids=list(range(len(jobs))))
    out = np.zeros((B, 2 * NT, D), np.float32)
    for (b, s), r in zip(jobs, res.results):
        out[b, s * NT:(s + 1) * NT] = np.asarray(r["outT"], np.float32).T
    return out
```

```python
import numpy as np
from contextlib import ExitStack
import concourse.bass as bass
import concourse.mybir as mybir
from concourse.bass_utils import run_bass_kernel_spmd

F32 = mybir.dt.float32
BF16 = mybir.dt.bfloat16
AF = mybir.ActivationFunctionType
ALU = mybir.AluOpType

D = 2048
KC = 16
NT = 1024
TT = 512
DFF = 5632
EPS = 1e-6
ENGS = ("pe", "act", "dve", "pool", "sp")


class Buf:
    __slots__ = ("name", "t", "w_ev", "r_evs", "dsem", "dcount")

    def __init__(self, name, t):
        self.name = name
        self.t = t
        self.w_ev = None
        self.r_evs = []
        self.dsem = None
        self.dcount = 0

    def __getitem__(self, idx):
        return self.t[idx]


class Sched:
    def __init__(self, nc, stack):
        self.nc = nc
        self.stack = stack
        self.streams = {e: [] for e in ENGS}
        self.count = {e: 0 for e in ENGS}
        self.seen = {e: {} for e in ENGS}
        self.sems = {}
        for e in ENGS:
            self.sems[e] = stack.enter_context(nc.semaphore("s_" + e))
        self.nd = 0
        self.pending_sig = {e: False for e in ENGS}
        self.latest = {}

    def sb(self, name, shape, dtype, stack=None):
        t = (stack or self.stack).enter_context(self.nc.sbuf_tensor("sb_" + name, list(shape), dtype))
        return Buf(name, t)

    def ps(self, name, shape, dtype=F32):
        t = self.stack.enter_context(self.nc.psum_tensor("pp_" + name, list(shape), dtype))
        return Buf(name, t)

    def dram(self, name, ap):
        return Buf(name, ap)

    def _dsem(self, b):
        if b.dsem is None:
            self.nd += 1
            key = "d%d" % self.nd
            self.sems[key] = self.stack.enter_context(self.nc.semaphore(key))
            b.dsem = key
        return b.dsem

    def _need(self, eng, ev):
        if ev is None:
            return
        key, val = ev
        if key == eng and (eng == "pe" or val > self.count[eng]):
            return
        if self.seen[eng].get(key, 0) >= val:
            return
        self.seen[eng][key] = val
        self.streams[eng].append(("wait", key, val))

    def _deps(self, eng, reads, writes):
        for b in reads:
            self._need(eng, b.w_ev)
        for b in writes:
            self._need(eng, b.w_ev)
            for ev in b.r_evs:
                self._need(eng, ev)

    def _record(self, ev, reads, writes):
        self.latest[ev[0]] = max(self.latest.get(ev[0], 0), ev[1])
        for b in reads:
            if not b.r_evs or b.r_evs[-1] != ev:
                b.r_evs.append(ev)
        for b in writes:
            b.w_ev = ev
            b.r_evs = []

    def op(self, eng, fn, reads=(), writes=(), sig=True):
        self._deps(eng, reads, writes)
        if sig:
            self.count[eng] += 1
            ev = (eng, self.count[eng])
            self.streams[eng].append(("op", fn, eng, 1))
            self.pending_sig[eng] = False
        else:
            ev = (eng, self.count[eng] + 1)
            self.streams[eng].append(("op", fn, None, 0))
            self.pending_sig[eng] = True
        self._record(ev, reads, writes)
        return ev

    def dmaop(self, q, fn, reads=(), writes=()):
        assert len(writes) == 1
        self._deps(q, reads, writes)
        wb = writes[0]
        key = self._dsem(wb)
        wb.dcount += 16
        ev = (key, wb.dcount)
        self.streams[q].append(("op", fn, key, 16))
        self._record(ev, reads, writes)
        return ev

    def dma(self, q, out_ap, in_ap, reads=(), writes=()):
        return self.dmaop(q, lambda e: e.dma_start(out=out_ap, in_=in_ap), reads, writes)

    def barrier(self):
        for e in ENGS:
            assert not self.pending_sig[e]
        for e in ENGS:
            for key, val in self.latest.items():
                self._need(e, (key, val))

    def emit(self):
        nc = self.nc
        for e in ENGS:
            assert not self.pending_sig[e], "engine %s ends with unsignalled op" % e
        streams = self.streams
        self.streams = {e: [] for e in ENGS}
        if DBG.get('sim'):
            vals = self.__dict__.setdefault('simvals', {})
            pc = {e: 0 for e in ENGS}
            prog = True
            while prog:
                prog = False
                for e in ENGS:
                    st_ = streams[e]
                    while pc[e] < len(st_):
                        it = st_[pc[e]]
                        if it[0] == "wait":
                            if vals.get(it[1], 0) >= it[2]:
                                pc[e] += 1
                                prog = True
                            else:
                                break
                        else:
                            if it[2] is not None:
                                vals[it[2]] = vals.get(it[2], 0) + it[3]
                            pc[e] += 1
                            prog = True
            for e in ENGS:
                if pc[e] < len(streams[e]):
                    it = streams[e][pc[e]]
                    print("SIM DEADLOCK: engine", e, "stuck at", pc[e], "/", len(streams[e]), it[:3], "have", vals.get(it[1], 0))
            print("SIM block done", {e: len(streams[e]) for e in ENGS})
            return
        with nc.Block() as block:
            def run(engname):
                def body(eng):
                    for item in streams[engname]:
                        if item[0] == "wait":
                            eng.wait_ge(self.sems[item[1]], item[2])
                        else:
                            _, fn, key, inc = item
                            ins = fn(eng)
                            if key is not None:
                                ins.then_inc(self.sems[key], inc)
                return body
            block.tensor(run("pe"))
            block.scalar(run("act"))
            block.vector(run("dve"))
            block.gpsimd(run("pool"))
            block.sync(run("sp"))


DBG = {}


class K:
    pass


def MM(S, out_ap, lhsT, rhs, start, stop, reads, writes, sig=None):
    if sig is None:
        sig = stop
    S.op("pe", lambda e: e.matmul(out_ap, lhsT=lhsT, rhs=rhs, start=start, stop=stop), reads, writes, sig)


def ACT(S, out_ap, in_ap, func, reads, writes, bias=None, scale=None):
    kw = {}
    if bias is not None:
        kw["bias"] = bias
    if scale is not None:
        kw["scale"] = scale
    S.op("act", lambda e: e.activation(out=out_ap, in_=in_ap, func=func, **kw), reads, writes)


def TT_(S, out_ap, in0, in1, op, reads, writes, eng="dve"):
    S.op(eng, lambda e: e.tensor_tensor(out=out_ap, in0=in0, in1=in1, op=op), reads, writes)


def TS(S, out_ap, in0, s1, s2, op0, op1, reads, writes, eng="dve"):
    if s2 is None:
        S.op(eng, lambda e: e.tensor_scalar(out=out_ap, in0=in0, scalar1=s1, scalar2=None, op0=op0), reads, writes)
    else:
        S.op(eng, lambda e: e.tensor_scalar(out=out_ap, in0=in0, scalar1=s1, scalar2=s2, op0=op0, op1=op1), reads, writes)


def STT(S, out_ap, in0, scalar, in1, op0, op1, reads, writes, eng="dve"):
    S.op(eng, lambda e: e.scalar_tensor_tensor(out=out_ap, in0=in0, scalar=scalar, in1=in1, op0=op0, op1=op1), reads, writes)


def CP(S, out_ap, in_ap, reads, writes, eng="dve"):
    if eng == "act":
        S.op("act", lambda e: e.copy(out=out_ap, in_=in_ap), reads, writes)
    else:
        S.op(eng, lambda e: e.tensor_copy(out=out_ap, in_=in_ap), reads, writes)


class Rot:
    def __init__(self, items):
        self.items = items
        self.i = 0

    def next(self):
        b = self.items[self.i % len(self.items)]
        self.i += 1
        return b


def wload(k, view_fn, dram_ap):
    slot = k.wring.next()
    k.S.dma("pool", view_fn(slot), dram_ap, writes=[slot])
    return slot


def v3(n1, n2):
    return lambda slot: slot[:, 0:n1 * n2].rearrange("p (a b) -> p a b", a=n1)


def rstd_from_ps(k, ps, n, width, rs):
    S = k.S
    ACT(S, rs[:, 0:width], ps[:, 0:width], AF.Ln, [ps, k.epscol], [rs], bias=k.epscol[:, 0:1], scale=1.0 / n)
    ACT(S, rs[:, 0:width], rs[:, 0:width], AF.Exp, [rs], [rs], scale=-0.5)


def recip_act(k, out_ap, in_ap, reads, writes):
    S = k.S
    ACT(S, out_ap, in_ap, AF.Ln, reads, writes)
    ACT(S, out_ap, out_ap, AF.Exp, writes, writes, scale=-1.0)


def rmsnorm_fm(k, x_ap_fn, xbufs, gcol, out_ap_fn, outbufs, ntok):
    S = k.S
    for lo in range(0, ntok, TT):
        hi = min(lo + TT, ntok)
        w = hi - lo
        ps = k.psum.next()
        for kc in range(KC):
            sq = k.sqring.next()
            ACT(S, sq[:, 0:w], x_ap_fn(kc, lo, hi), AF.Square, xbufs(kc, lo), [sq])
            MM(S, ps[:, 0:w], k.ones[:, :], sq[:, 0:w], kc == 0, kc == KC - 1, [sq, k.ones], [ps], sig=True)
        rs = k.rsring.next()
        rstd_from_ps(k, ps, float(D), w, rs)
        for kc in range(KC):
            STT(S, out_ap_fn(kc, lo, hi), x_ap_fn(kc, lo, hi), k.gains[:, gcol + kc:gcol + kc + 1], rs[:, 0:w],
                ALU.mult, ALU.mult, xbufs(kc, lo) + [rs, k.gains], outbufs(kc, lo))


def ffn(k, wg, wu, wd):
    S = k.S
    xT, hA = k.xT, k.hA
    nblk = DBG.get('nblk', DFF // 512)
    for blk in range(nblk):
        c0 = blk * 512
        gs = [wload(k, v3(16, 256), wg[:, c0 + i * 256:c0 + (i + 1) * 256].rearrange("(kc p) c -> p kc c", p=128)) for i in range(2)]
        us = [wload(k, v3(16, 256), wu[:, c0 + i * 256:c0 + (i + 1) * 256].rearrange("(kc p) c -> p kc c", p=128)) for i in range(2)]
        aT = k.aT.next()
        for mc in range(4):
            gsl = v3(16, 256)(gs[mc // 2])
            usl = v3(16, 256)(us[mc // 2])
            off = (mc % 2) * 128
            psg = [k.psum.next() for _ in range(2)]
            for kc in range(KC):
                for tt in range(2):
                    MM(S, psg[tt][:, :], gsl[:, kc, off:off + 128], hA[:, kc, tt * TT:(tt + 1) * TT], kc == 0, kc == KC - 1,
                       [gs[mc // 2], hA], [psg[tt]])
            psu = [k.psum.next() for _ in range(2)]
            for kc in range(KC):
                for tt in range(2):
                    MM(S, psu[tt][:, :], usl[:, kc, off:off + 128], hA[:, kc, tt * TT:(tt + 1) * TT], kc == 0, kc == KC - 1,
                       [us[mc // 2], hA], [psu[tt]])
            for tt in range(2):
                sg = k.sgring.next()
                ACT(S, sg[:, :], psg[tt][:, :], AF.Silu, [psg[tt]], [sg])
                TT_(S, aT[:, mc, tt * TT:(tt + 1) * TT], sg[:, :], psu[tt][:, :], ALU.mult, [sg, psu[tt]], [aT])
        ds = [wload(k, v3(4, 1024), wd[c0:c0 + 512, i * 1024:(i + 1) * 1024].rearrange("(kc p) c -> p kc c", p=128)) for i in range(2)]
        for m in range(KC):
            dsl = v3(4, 1024)(ds[m // 8])
            off = (m % 8) * 128
            pss = [k.psum.next() for _ in range(2)]
            for kc in range(4):
                for tt in range(2):
                    MM(S, pss[tt][:, :], dsl[:, kc, off:off + 128], aT[:, kc, tt * TT:(tt + 1) * TT], kc == 0, kc == 3,
                       [ds[m // 8], aT], [pss[tt]])
            for tt in range(2):
                xb = xT[m][tt]
                STT(S, xb[:, :], pss[tt][:, :], 0.5, xb[:, :], ALU.mult, ALU.add, [pss[tt], xb], [xb])


def x_fn(k):
    return (lambda kc, lo, hi: k.xT[kc][lo // TT][:, 0:hi - lo]), (lambda kc, lo: [k.xT[kc][lo // TT]])


def h_fn(hbuf):
    return (lambda kc, lo, hi: hbuf[:, kc, lo:hi]), (lambda kc, lo: [hbuf])


def load_x(k, x_dram):
    S = k.S
    xv = x_dram.rearrange("(kc p) n -> p kc n", p=128)
    for kc in range(KC):
        for tt in range(2):
            S.dma("sp", k.xT[kc][tt][:, :], xv[:, kc, tt * TT:(tt + 1) * TT], writes=[k.xT[kc][tt]])


def headnorm(k, src, width, gcol, dst_f32):
    S = k.S
    for lo in range(0, width, TT):
        w = min(TT, width - lo)
        sq = k.sqring.next()
        ACT(S, sq[:, 0:w], src[:, lo:lo + w], AF.Square, [src], [sq])
        ps = k.psum.next()
        MM(S, ps[:, 0:w], k.ones[:, :], sq[:, 0:w], True, True, [sq, k.ones], [ps])
        rs = k.rsring.next()
        rstd_from_ps(k, ps, 128.0, w, rs)
        STT(S, dst_f32[:, lo:lo + w], src[:, lo:lo + w], k.gains[:, gcol:gcol + 1], rs[:, 0:w], ALU.mult, ALU.mult,
            [src, rs, k.gains], [dst_f32])


def rope(k, xn, width, tok0, dst_bf):
    S = k.S
    for lo in range(0, width, TT):
        w = min(TT, width - lo)
        ps = k.psum.next()
        MM(S, ps[:, 0:w], k.perm, xn[:, lo:lo + w], True, True, [xn, k.mats], [ps])
        t1 = k.rsring.next()
        TT_(S, t1[:, 0:w], xn[:, lo:lo + w], k.cos2[:, tok0 + lo:tok0 + lo + w], ALU.mult, [xn, k.cos2], [t1])
        t2 = k.rsring.next()
        TT_(S, t2[:, 0:w], ps[:, 0:w], k.sin2[:, tok0 + lo:tok0 + lo + w], ALU.mult, [ps, k.sin2], [t2])
        TT_(S, dst_bf[:, lo:lo + w], t1[:, 0:w], t2[:, 0:w], ALU.add, [t1, t2], [dst_bf])


def proj_fm(k, wslot, wsl, col0, srcs, dst_f32):
    S = k.S
    o = 0
    for hbuf, ntok in srcs:
        for lo in range(0, ntok, TT):
            ps = k.psum.next()
            for kc in range(KC):
                MM(S, ps[:, :], wsl[:, kc, col0:col0 + 128], hbuf[:, kc, lo:lo + TT], kc == 0, kc == KC - 1, [wslot, hbuf], [ps])
            CP(S, dst_f32[:, o:o + TT], ps[:, :], [ps], [dst_f32], eng="act")
            o += TT


def qk_pipeline(k, tiles, fillers=None):
    S = k.S
    st = [dict() for _ in tiles]

    def stageA(i):
        wslot, wsl, col0, hbuf, lo, gcol, tok0, dst_bf, dst_lo = tiles[i]
        ps = k.psum.next()
        for kc in range(KC):
            MM(S, ps[:, :], wsl[:, kc, col0:col0 + 128], hbuf[:, kc, lo:lo + TT], kc == 0, kc == KC - 1, [wslot, hbuf], [ps])
        xf = k.rsring.next()
        CP(S, xf[:, :], ps[:, :], [ps], [xf], eng="act")
        sq = k.sqring.next()
        ACT(S, sq[:, :], xf[:, :], AF.Square, [xf], [sq])
        st[i]["xf"], st[i]["sq"] = xf, sq

    def stageB(i):
        wslot, wsl, col0, hbuf, lo, gcol, tok0, dst_bf, dst_lo = tiles[i]
        xf, sq = st[i]["xf"], st[i]["sq"]
        ps2 = k.psum.next()
        MM(S, ps2[:, :], k.ones[:, :], sq[:, :], True, True, [sq, k.ones], [ps2])
        rs = k.rsring.next()
        rstd_from_ps(k, ps2, 128.0, TT, rs)
        xn = k.rsring.next()
        STT(S, xn[:, :], xf[:, :], k.gains[:, gcol:gcol + 1], rs[:, :], ALU.mult, ALU.mult, [xf, rs, k.gains], [xn])
        xnb = k.sqring.next()
        CP(S, xnb[:, :], xn[:, :], [xn], [xnb], eng="act")
        st[i]["xn"], st[i]["xnb"] = xn, xnb

    def stageC(i):
        wslot, wsl, col0, hbuf, lo, gcol, tok0, dst_bf, dst_lo = tiles[i]
        xn, xnb = st[i]["xn"], st[i]["xnb"]
        ps3 = k.psum.next()
        MM(S, ps3[:, :], k.permb[:, :], xnb[:, :], True, True, [xnb, k.permb], [ps3])
        t1 = k.rsring.next()
        TT_(S, t1[:, :], xn[:, :], k.cos2[:, tok0:tok0 + TT], ALU.mult, [xn, k.cos2], [t1], eng="pool")
        t2 = k.rsring.next()
        TT_(S, t2[:, :], ps3[:, :], k.sin2[:, tok0:tok0 + TT], ALU.mult, [ps3, k.sin2], [t2])
        TT_(S, dst_bf[:, dst_lo:dst_lo + TT], t1[:, :], t2[:, :], ALU.add, [t1, t2], [dst_bf], eng="pool")

    n = len(tiles)
    fillers = list(fillers or [])
    for step in range(n + 2):
        if step < n:
            stageA(step)
        elif fillers:
            fillers.pop(0)()
        if 0 <= step - 1 < n:
            stageB(step - 1)
        if 0 <= step - 2 < n:
            stageC(step - 2)
    while fillers:
        fillers.pop(0)()


def attn_wload(k, h, w_in):
    f = v3(16, 128)
    wv = wload(k, f, w_in[:, 2048 + h * 128:2048 + (h + 1) * 128].rearrange("(kc p) c -> p kc c", p=128))
    wk = wload(k, f, w_in[:, 1024 + h * 128:1024 + (h + 1) * 128].rearrange("(kc p) c -> p kc c", p=128))
    wq = wload(k, f, w_in[:, h * 128:(h + 1) * 128].rearrange("(kc p) c -> p kc c", p=128))
    return wq, wk, wv


def attn_head(k, h, w_in, st):
    S = k.S
    hC, hH = k.hC, k.hA
    if getattr(k, "attn_w", None) is None:
        k.attn_w = attn_wload(k, h, w_in)
    wq, wk, wv = k.attn_w
    k.attn_w = None
    wqs, wks, wvs = v3(16, 128)(wq), v3(16, 128)(wk), v3(16, 128)(wv)
    kT, qT = k.kT, k.qT
    tiles = []
    for i, (hb, lo) in enumerate(((hC, 0), (hC, TT), (hH, 0), (hH, TT))):
        tiles.append((wk, wks, 0, hb, lo, k.G_AK, i * TT, kT, i * TT))
    for i in range(2):
        tiles.append((wq, wqs, 0, hH, i * TT, k.G_AQ, 1024 + i * TT, qT, i * TT))
    def vgroup(hb, lo, o):
        def f():
            ps = k.psum.next()
            for kc in range(KC):
                MM(S, ps[:, :], wvs[:, kc, 0:128], hb[:, kc, lo:lo + TT], kc == 0, kc == KC - 1, [wv, hb], [ps])
            CP(S, k.vT[:, o:o + TT], ps[:, :], [ps], [k.vT], eng="act")
        return f
    vfill = [vgroup(hC, 0, 0), vgroup(hC, TT, TT), vgroup(hH, 0, 2 * TT), vgroup(hH, TT, 3 * TT)]
    vfill.pop(0)()
    vfill.pop(0)()
    qk_pipeline(k, tiles, vfill)
    if h < 7:
        k.attn_w = attn_wload(k, h + 1, w_in)

    V1, V4, V16, vT = k.V1, k.V4, k.V16, k.vT
    tjobs = []
    for j in range(7, 16):
        tjobs.append((V1, j - 7, 128 * j, 1))
    for c in range(4):
        for j in range(1, 4):
            tjobs.append((V4, c * 3 + j - 1, c + 4 * 128 * j, 4))
    for c in range(16):
        tjobs.append((V16, c, c, 16))
    for g0 in range(0, len(tjobs), 4):
        grp = tjobs[g0:g0 + 4]
        ps = k.psum.next()
        for i, (db, di, t0, stp) in enumerate(grp):
            S.op("pe", lambda e, ps=ps, i=i, t0=t0, stp=stp: e.transpose(ps[:, i * 128:(i + 1) * 128],
                                                                          vT[:, t0:t0 + stp * 127 + 1:stp], k.ident),
                 [vT, k.mats], [ps])
        db0, di0 = grp[0][0], grp[0][1]
        same = all(g[0] is db0 for g in grp) and all(grp[i][1] == di0 + i for i in range(len(grp)))
        if same:
            n = len(grp)
            CP(S, db0[:, di0:di0 + n, :], ps[:, 0:n * 128].rearrange("p (a b) -> p a b", a=n), [ps], [db0], eng="act")
        else:
            for i, (db, di, t0, stp) in enumerate(grp):
                CP(S, db[:, di, :], ps[:, i * 128:(i + 1) * 128], [ps], [db], eng="act")

    OD = k.OD
    sc = 128.0 ** -0.5
    M_same, M_next, M_nextC, M16 = k.M_same, k.M_next, k.M_nextC, k.M16

    def keys_ap(r, c, j, n=128, m_lo=0):
        start = c + r * (128 * j + m_lo)
        return kT[:, start:start + r * (n - 1) + 1:r]

    def q_ap(r, c, i, n=128, m_lo=0):
        start = c + r * (128 * i + m_lo) - 1024
        return qT[:, start:start + r * (n - 1) + 1:r]

    def acc_ap(r, c, i, n=128, m_lo=0):
        start = c + r * (128 * i + m_lo) - 1024
        return OD[:, :, start:start + r * (n - 1) + 1:r]

    sjobs = []
    pjobs = []
    s0 = len(sjobs)
    sjobs.append((1, 0, 7, [(8, M_nextC, 128, 0)]))
    for j in range(8, 16):
        parts = [(j, M_same, 128, 0)]
        if j < 15:
            parts.append((j + 1, M_next, 128, 0))
        sjobs.append((1, 0, j, parts))
        pjobs.append((1, 0, j, [(V1, V1[:, j - 1 - 7, :], s0 + j - 8, 0 if j == 8 else 128), (V1, V1[:, j - 7, :], s0 + j - 7, 0)], 128, 0, True))
    for c in range(4):
        s0 = len(sjobs)
        sjobs.append((4, c, 1, [(2, M_nextC, 128, 0)]))
        sjobs.append((4, c, 2, [(2, M_same, 128, 0), (3, M_next, 128, 0)]))
        sjobs.append((4, c, 3, [(3, M_same, 128, 0)]))
        pjobs.append((4, c, 2, [(V4, V4[:, c * 3 + 0, :], s0, 0), (V4, V4[:, c * 3 + 1, :], s0 + 1, 0)], 128, 0, False))
        pjobs.append((4, c, 3, [(V4, V4[:, c * 3 + 1, :], s0 + 1, 128), (V4, V4[:, c * 3 + 2, :], s0 + 2, 0)], 128, 0, False))
    for c in range(16):
        s0 = len(sjobs)
        sjobs.append((16, c, 0, [(0, M16, 64, 64)]))
        pjobs.append((16, c, 0, [(V16, V16[:, c, :], s0, 0)], 64, 64, False))

    Ps = {}
    state = {"next": 0}

    def emit_scores(upto):
        while state["next"] <= min(upto, len(sjobs) - 1):
            r, c, j, qparts = sjobs[state["next"]]
            ps = k.psum.next()
            P = k.Pring.next()
            o = 0
            for (i, mask, nq, m_lo) in qparts:
                MM(S, ps[:, o:o + nq], keys_ap(r, c, j), q_ap(r, c, i, nq, m_lo), True, True, [kT, qT], [ps])
                o += nq
            E = k.Ering.next()
            ACT(S, E[:, 0:o], ps[:, 0:o], AF.Exp, [ps], [E], scale=sc)
            o = 0
            for (i, mask, nq, m_lo) in qparts:
                TT_(S, P[:, o:o + nq], E[:, o:o + nq], mask, ALU.mult, [E, k.masks], [P], eng="pool")
                o += nq
            Ps[state["next"]] = P
            state["next"] += 1

    LOOK = 7
    for (r, c, i, contribs, nq, m_lo, first) in pjobs:
        emit_scores(max(ci[2] for ci in contribs) + LOOK)
        ps = k.psum.next()
        n = len(contribs)
        for idx, (vb, vap, sidx, po) in enumerate(contribs):
            P = Ps[sidx]
            MM(S, ps[:, 0:nq], vap, P[:, po:po + nq], idx == 0, idx == n - 1, [vb, P], [ps], sig=False)
        for idx, (vb, vap, sidx, po) in enumerate(contribs):
            P = Ps[sidx]
            MM(S, ps[:, 128:128 + nq], k.ones[:, :], P[:, po:po + nq], idx == 0, idx == n - 1, [k.ones, P], [ps], sig=(idx == n - 1))
        psv = ps[:, 0:256].rearrange("p (a b) -> p a b", a=2)[:, :, 0:nq]
        if first:
            CP(S, acc_ap(r, c, i, nq, m_lo), psv, [ps], [OD])
        else:
            TT_(S, acc_ap(r, c, i, nq, m_lo), acc_ap(r, c, i, nq, m_lo), psv, ALU.add, [ps, OD], [OD])
    recip_act(k, OD[:, 1, :], OD[:, 1, :], [OD], [OD])
    TT_(S, k.oT[:, h, :], OD[:, 0, :], OD[:, 1, :], ALU.mult, [OD], [k.oT])


def gla_prep(k, w_in):
    S = k.S
    wsl_fn = v3(16, 16)
    wg = wload(k, wsl_fn, w_in[:, 6144:6160].rearrange("(kc p) c -> p kc c", p=128))
    wsl = wsl_fn(wg)
    S.op("dve", lambda e: e.memset(k.glrT[:, :], 1.0), [], [k.glrT])
    o = 0
    for hb in (k.hC, k.hA):
        for lo in range(0, NT, TT):
            ps = k.psum.next()
            for kc in range(KC):
                MM(S, ps[0:16, :], wsl[:, kc, :], hb[:, kc, lo:lo + TT], kc == 0, kc == KC - 1, [wg, hb], [ps])
            CP(S, k.glrT[0:16, o:o + TT], ps[0:16, :], [ps], [k.glrT], eng="act")
            o += TT


def gla_head(k, g, w_in):
    S = k.S
    hC, hH = k.hC, k.hA
    QG0, KG0, VG0, RG0 = 3072, 3584, 4096, 5120
    f128 = v3(16, 128)
    f256 = v3(16, 256)
    wq = wload(k, f128, w_in[:, QG0 + g * 128:QG0 + (g + 1) * 128].rearrange("(kc p) c -> p kc c", p=128))
    wk = wload(k, f128, w_in[:, KG0 + g * 128:KG0 + (g + 1) * 128].rearrange("(kc p) c -> p kc c", p=128))
    wv = wload(k, f256, w_in[:, VG0 + g * 256:VG0 + (g + 1) * 256].rearrange("(kc p) c -> p kc c", p=128))
    kf, lbuf, cl = k.kf, k.lbuf, k.cl
    proj_fm(k, wk, f128(wk), 0, [(hC, NT), (hH, NT)], kf)
    for lo in range(0, 2048, TT):
        ps = k.psum.next()
        MM(S, ps[:, :], k.w2[0:17, g * 128:(g + 1) * 128], k.glrT[0:17, lo:lo + TT], True, True, [k.w2, k.glrT], [ps])
        e1 = k.rsring.next()
        ACT(S, e1[:, :], ps[:, :], AF.Exp, [ps], [e1], scale=-1.0)
        ACT(S, lbuf[:, lo:lo + TT], e1[:, :], AF.Ln, [e1], [lbuf], bias=k.onecol[:, 0:1])
        S.op("dve", lambda e, lo=lo: e.tensor_tensor_scan(out=cl[:, lo:lo + TT], data0=k.rmask[:, :], data1=lbuf[:, lo:lo + TT],
                                                           initial=0.0, op0=ALU.mult, op1=ALU.add), [k.rmask, lbuf], [cl])
    for i in range(2):
        lo = i * TT
        ps = k.psum.next()
        for kc in range(KC):
            MM(S, ps[:, :], f128(wq)[:, kc, :], hH[:, kc, lo:lo + TT], kc == 0, kc == KC - 1, [wq, hH], [ps])
        eb = k.rsring.next()
        ACT(S, eb[:, :], cl[:, 1024 + lo:1024 + lo + TT], AF.Exp, [cl], [eb], scale=-1.0 / 16)
        STT(S, k.qT[:, lo:lo + TT], ps[:, :], 128.0 ** -0.5, eb[:, :], ALU.mult, ALU.mult, [ps, eb], [k.qT])
        enb = k.rsring.next()
        ACT(S, enb[:, :], cl[:, 1024 + lo:1024 + lo + TT], AF.Exp, [cl], [enb], scale=1.0 / 16)
        TT_(S, k.keT[:, lo:lo + TT], kf[:, 1024 + lo:1024 + lo + TT], enb[:, :], ALU.mult, [kf, enb], [k.keT])
    dtot = k.dtot
    ACT(S, dtot[:, :], cl[:, 63:2048:64], AF.Exp, [cl], [dtot], scale=-1.0 / 16)
    dl = lbuf
    clv = cl[:, :].rearrange("p (n c) -> p n c", c=64)
    S.op("dve", lambda e: e.tensor_tensor(out=dl[:, :].rearrange("p (n c) -> p n c", c=64), in0=clv[:, :, 63:64].to_broadcast([128, 32, 64]),
                                          in1=clv, op=ALU.subtract), [cl], [dl])
    ACT(S, dl[:, :], dl[:, :], AF.Exp, [dl], [dl], scale=-1.0 / 16)
    TT_(S, kf[:, :], kf[:, :], dl[:, :], ALU.mult, [kf, dl], [kf])
    kdecT = kf
    Sfr, Sbr = k.Sfr, k.Sbr
    Sf = Sfr.next()
    S.op("dve", lambda e, Sf0=Sf: e.memset(Sf0[:, :], 0.0), [], [Sf])
    Sb = None
    og = k.og
    for blk in range(16):
        hb, lo = (hC, blk * 128) if blk < 8 else (hH, (blk - 8) * 128)
        ps = k.psum.next()
        for kc in range(KC):
            MM(S, ps[:, 0:256], hb[:, kc, lo:lo + 128], f256(wv)[:, kc, :], kc == 0, kc == KC - 1, [wv, hb], [ps])
        vt = k.vtring.next()
        CP(S, vt[:, :], ps[:, 0:256], [ps], [vt], eng="act")
        pst = k.psum.next()
        S.op("pe", lambda e, pst=pst, blk=blk: e.transpose(pst[:, 0:128], kdecT[:, blk * 128:(blk + 1) * 128], k.ident),
             [kdecT, k.mats], [pst])
        kd = k.kdring.next()
        CP(S, kd[:, :], pst[:, 0:128], [pst], [kd])
        own = blk >= 8
        Sb_before = [Sb, None]
        for ch in range(2):
            n = blk * 2 + ch
            p0 = ch * 64
            psd = k.psum.next()
            MM(S, psd[:, 0:256], kd[p0:p0 + 64, :], vt[p0:p0 + 64, :], True, True, [kd, vt], [psd])
            Sn = Sfr.next()
            STT(S, Sn[:, :], Sf[:, :], dtot[:, n:n + 1], psd[:, 0:256], ALU.mult, ALU.add, [Sf, dtot, psd], [Sn])
            Sf = Sn
            if n >= 15 and n < 31:
                Sb = Sbr.next()
                CP(S, Sb[:, :], Sf[:, :], [Sf], [Sb], eng="act")
            if ch == 0:
                Sb_before[1] = Sb
        if own:
            t0 = (blk - 8) * 128
            psa = k.psum.next()
            MM(S, psa[:, 0:128], k.keT[:, t0:t0 + 128], k.qT[:, t0:t0 + 128], True, True, [k.keT, k.qT], [psa])
            At = k.Pring.next()
            TT_(S, At[:, 0:128], psa[:, 0:128], k.MG, ALU.mult, [psa, k.masks], [At])
            pso = [k.psum.next() for _ in range(2)]
            for ch in range(2):
                p0 = ch * 64
                tq = (blk - 8) * 128 + ch * 64
                Sbb = Sb_before[ch]
                for dc in range(2):
                    MM(S, pso[dc][:, p0:p0 + 64], vt[:, dc * 128:(dc + 1) * 128], At[:, p0:p0 + 64], True, False, [vt, At], [pso[dc]], sig=False)
                    MM(S, pso[dc][:, p0:p0 + 64], Sbb[:, dc * 128:(dc + 1) * 128], k.qT[:, tq:tq + 64], False, True, [Sbb, k.qT], [pso[dc]],
                       sig=True)
            for dc in range(2):
                CP(S, og[:, dc, t0:t0 + 128], pso[dc][:, 0:128], [pso[dc]], [og], eng="act")
    wr = wload(k, f256, w_in[:, RG0 + g * 256:RG0 + (g + 1) * 256].rearrange("(kc p) c -> p kc c", p=128))
    for lo in range(0, NT, TT):
        ps = k.psum.next()
        for dc in range(2):
            sq = k.sqring.next()
            ACT(S, sq[:, :], og[:, dc, lo:lo + TT], AF.Square, [og], [sq])
            MM(S, ps[:, :], k.ones[:, :], sq[:, :], dc == 0, dc == 1, [sq, k.ones], [ps], sig=True)
        rs = k.rsring.next()
        rstd_from_ps(k, ps, 256.0, TT, rs)
        for dc in range(2):
            psr = k.psum.next()
            for kc in range(KC):
                MM(S, psr[:, :], f256(wr)[:, kc, dc * 128:(dc + 1) * 128], hH[:, kc, lo:lo + TT], kc == 0, kc == KC - 1, [wr, hH], [psr])
            sg = k.rsring.next()
            ACT(S, sg[:, :], psr[:, :], AF.Silu, [psr], [sg])
            t1 = k.rsring.next()
            STT(S, t1[:, :], og[:, dc, lo:lo + TT], k.gains[:, k.G_GO + dc:k.G_GO + dc + 1], rs[:, :], ALU.mult, ALU.mult,
                [og, rs, k.gains], [t1])
            TT_(S, k.oT[:, 8 + 2 * g + dc, lo:lo + TT], t1[:, :], sg[:, :], ALU.mult, [t1, sg], [k.oT])


def add_proj(k, w, nkc, src_bf):
    S = k.S
    for mp in range(8):
        fn = v3(nkc, 256)
        sl = wload(k, fn, w[:, mp * 256:(mp + 1) * 256].rearrange("(kc p) c -> p kc c", p=128))
        for mi in range(2):
            m = mp * 2 + mi
            pss = [k.psum.next() for _ in range(2)]
            for kc in range(nkc):
                for tt in range(2):
                    MM(S, pss[tt][:, :], fn(sl)[:, kc, mi * 128:(mi + 1) * 128], src_bf[:, kc, tt * TT:(tt + 1) * TT], kc == 0, kc == nkc - 1,
                       [sl, src_bf], [pss[tt]])
            for tt in range(2):
                xb = k.xT[m][tt]
                TT_(S, xb[:, :], xb[:, :], pss[tt][:, :], ALU.add, [pss[tt], xb], [xb])


def xattn_mem(k, wk, wv, memT_d):
    S = k.S
    memf, mT = k.memf, k.mT
    rmsnorm_fm(k, lambda kc, lo, hi: memf[:, kc, lo:hi], lambda kc, lo: [memf], k.G_MEM,
               lambda kc, lo, hi: mT[:, kc, lo:hi], lambda kc, lo: [mT], 256)
    f256 = v3(16, 256)
    kx = k.kx
    for hp in range(2):
        sl = wload(k, f256, wk[:, hp * 256:(hp + 1) * 256].rearrange("(kc p) c -> p kc c", p=128))
        for hi_ in range(2):
            hx = hp * 2 + hi_
            ps = k.psum.next()
            for kc in range(KC):
                MM(S, ps[:, 0:256], f256(sl)[:, kc, hi_ * 128:(hi_ + 1) * 128], mT[:, kc, :], kc == 0, kc == KC - 1, [sl, mT], [ps])
            CP(S, k.kf[:, 0:256], ps[:, 0:256], [ps], [k.kf], eng="act")
            headnorm(k, k.kf, 256, k.G_XK, k.kn)
            CP(S, kx[:, hx, :], k.kn[:, 0:256], [k.kn], [kx])
    vx = k.vx
    sls = [wload(k, f256, wv[:, i * 256:(i + 1) * 256].rearrange("(kc p) c -> p kc c", p=128)) for i in range(2)]
    for blk in range(2):
        ps = k.psum.next()
        for i in range(2):
            for kc in range(KC):
                MM(S, ps[:, i * 256:(i + 1) * 256], mT[:, kc, blk * 128:(blk + 1) * 128], f256(sls[i])[:, kc, :], kc == 0, kc == KC - 1,
                   [sls[i], mT], [ps], sig=(kc == KC - 1))
        CP(S, vx[:, blk, :], ps[:, :], [ps], [vx], eng="act")


def xattn(k, wq, wo):
    S = k.S
    kx, vx = k.kx, k.vx
    f256 = v3(16, 256)
    sc = 128.0 ** -0.5
    for hp in range(2):
        sl = wload(k, f256, wq[:, hp * 256:(hp + 1) * 256].rearrange("(kc p) c -> p kc c", p=128))
        for hi_ in range(2):
            hx = hp * 2 + hi_
            proj_fm(k, sl, f256(sl), hi_ * 128, [(k.hA, NT)], k.qf)
            headnorm(k, k.qf, 1024, k.G_XQ, k.qn)
            CP(S, k.qT[:, :], k.qn[:, :], [k.qn], [k.qT])
            for tt in range(2):
                Es = []
                for blk in range(2):
                    ps = k.psum.next()
                    MM(S, ps[:, :], kx[:, hx, blk * 128:(blk + 1) * 128], k.qT[:, tt * TT:(tt + 1) * TT], True, True, [kx, k.qT], [ps])
                    E = k.Ering.next()
                    ACT(S, E[:, :], ps[:, :], AF.Exp, [ps], [E], scale=sc)
                    Es.append(E)
                pso = k.psum.next()
                psd = k.psum.next()
                for blk in range(2):
                    MM(S, pso[:, :], vx[:, blk, hx * 128:(hx + 1) * 128], Es[blk][:, :], blk == 0, blk == 1, [vx, Es[blk]], [pso])
                for blk in range(2):
                    MM(S, psd[:, :], k.ones[:, :], Es[blk][:, :], blk == 0, blk == 1, [k.ones, Es[blk]], [psd])
                rd = k.rsring.next()
                recip_act(k, rd[:, :], psd[:, :], [psd], [rd])
                TT_(S, k.oT[:, hx, tt * TT:(tt + 1) * TT], pso[:, :], rd[:, :], ALU.mult, [pso, rd], [k.oT])
    add_proj(k, wo, 4, k.oT)


NG = 96
G_FFN1, G_MIX, G_XA, G_MEM, G_FFN2 = 0, 16, 32, 48, 64
G_AQ, G_AK, G_XQ, G_XK, G_GO, G_B2 = 80, 81, 82, 83, 84, 86
NMASK = 512


def wout_to_dram(k, w, src_bf, xin_d, xin_buf, xout_d, xout_buf):
    S = k.S
    xi = xin_d.rearrange("(kc p) n -> p kc n", p=128)
    xo = xout_d.rearrange("(kc p) n -> p kc n", p=128)
    fn = v3(16, 256)
    for mp in range(8):
        sl = wload(k, fn, w[:, mp * 256:(mp + 1) * 256].rearrange("(kc p) c -> p kc c", p=128))
        for mi in range(2):
            m = mp * 2 + mi
            pss = [k.psum.next() for _ in range(2)]
            for kc in range(16):
                for tt in range(2):
                    MM(S, pss[tt][:, :], fn(sl)[:, kc, mi * 128:(mi + 1) * 128], src_bf[:, kc, tt * TT:(tt + 1) * TT], kc == 0, kc == 15,
                       [sl, src_bf], [pss[tt]])
            for tt in range(2):
                xs = k.stage.next()
                S.dma("sp", xs[:, :], xi[:, m, tt * TT:(tt + 1) * TT], reads=[xin_buf], writes=[xs])
                TT_(S, xs[:, :], xs[:, :], pss[tt][:, :], ALU.add, [pss[tt], xs], [xs])
                S.dma("sp", xo[:, m, tt * TT:(tt + 1) * TT], xs[:, :], reads=[xs], writes=[xout_buf])


def build_program(stage=99, debug=False):
    nc = bass.Bass("TRN2", target_bir_lowering=False)

    def din(name, shape, dt=F32):
        return nc.dram_tensor(name, list(shape), dt, kind="ExternalInput").ap()

    xC_d = din("xC", [D, NT])
    xH_d = din("xH", [D, NT])
    memT_d = din("memT", [D, 256])
    W = {}
    for nm, shp in (("ffn1_w_gate", [D, DFF]), ("ffn1_w_up", [D, DFF]), ("ffn1_w_down", [DFF, D]), ("w_in", [D, 6160]),
                    ("w_out", [D, D]), ("xattn_w_q", [D, 512]), ("xattn_w_k", [D, 512]), ("xattn_w_v", [D, 512]),
                    ("xattn_w_o", [512, D]), ("ffn2_w_gate", [D, DFF]), ("ffn2_w_up", [D, DFF]), ("ffn2_w_down", [DFF, D])):
        W[nm] = din(nm, shp)
    gains_d = din("gains", [128, NG])
    w2_d = din("w2aug", [17, 512])
    cos_d = din("cos2", [128, 2048])
    sin_d = din("sin2", [128, 2048])
    masks_d = din("masks", [128, NMASK + 128])
    mats_d = din("mats", [128, 256])
    out_d = nc.dram_tensor("outT", [D, NT], F32, kind="ExternalOutput").ap()
    x1_d = nc.dram_tensor("x1_scratch", [D, NT], F32, kind="ExternalOutput").ap()
    x2_d = nc.dram_tensor("x2_scratch", [D, NT], F32, kind="ExternalOutput").ap()

    with ExitStack() as st:
        S = Sched(nc, st)
        k = K()
        k.S = S
        k.nc = nc
        k.G_AQ, k.G_AK, k.G_XQ, k.G_XK, k.G_GO, k.G_MEM = G_AQ, G_AK, G_XQ, G_XK, G_GO, G_MEM
        k.psum = Rot([S.ps("ps%d" % i, [128, 512]) for i in range(8)])
        k.gains = S.sb("gains", [128, NG], F32)
        k.ones = S.sb("ones", [128, 128], BF16)
        k.onecol = S.sb("onecol", [128, 1], F32)
        k.masks = S.sb("masks", [128, NMASK + 128], BF16)
        k.M_same = k.masks[:, 0:128]
        k.M_next = k.masks[:, 128:256]
        k.M_nextC = k.masks[:, 256:384]
        k.M16 = k.masks[:, 384:448]
        k.MG = k.masks[:, NMASK:NMASK + 128]
        k.mats = S.sb("mats", [128, 256], F32)
        k.ident = k.mats[:, 0:128]
        k.perm = k.mats[:, 128:256]
        k.w2 = S.sb("w2", [32, 512], BF16)
        k.hA = S.sb("hA", [128, KC, NT], BF16)
        k.wring = Rot([S.sb("wr%d" % i, [128, 4096], BF16) for i in range(4)])
        k.sqring = Rot([S.sb("sq%d" % i, [128, TT], BF16) for i in range(5)])
        k.rsring = Rot([S.sb("rs%d" % i, [128, TT], F32) for i in range(8)])
        x1b = S.dram("x1d", x1_d)
        x2b = S.dram("x2d", x2_d)
        outb = S.dram("out", out_d)

        S.dma("sp", k.gains[:, :], gains_d, writes=[k.gains])
        S.dma("sp", k.mats[:, :], mats_d, writes=[k.mats])
        S.dma("pool", k.masks[:, :], masks_d, writes=[k.masks])
        S.op("dve", lambda e: e.memset(k.w2[:, :], 0.0), [], [k.w2])
        S.dma("pool", k.w2[0:17, :], w2_d, writes=[k.w2])
        S.op("dve", lambda e: e.memset(k.ones[:, :], 1.0), [], [k.ones])
        k.permb = S.sb("permb", [128, 128], BF16)
        CP(S, k.permb[:, :], k.perm, [k.mats], [k.permb])
        S.op("dve", lambda e: e.memset(k.onecol[:, :], 1.0), [], [k.onecol])
        k.epscol = S.sb("epscol", [128, 1], F32)
        S.op("dve", lambda e: e.memset(k.epscol[:, :], EPS), [], [k.epscol])

        def alloc_x(ph, tag):
            k.xT = [[S.sb("x%s%d_%d" % (tag, kc, tt), [128, TT], F32, ph) for tt in range(2)] for kc in range(KC)]

        def alloc_ffn(ph, tag):
            k.aT = Rot([S.sb("aT%s%d" % (tag, i), [128, 4, NT], BF16, ph) for i in range(1)])
            k.sgring = Rot([S.sb("sg%s%d" % (tag, i), [128, TT], F32, ph) for i in range(2)])

        def store_x(dst, dstbuf):
            dv = dst.rearrange("(kc p) n -> p kc n", p=128)
            for kc in range(KC):
                for tt in range(2):
                    S.dma("sp", dv[:, kc, tt * TT:(tt + 1) * TT], k.xT[kc][tt][:, :], reads=[k.xT[kc][tt]], writes=[dstbuf])

        def finish(ph_note=""):
            S.barrier()
            S.emit()

        with ExitStack() as ph12:
            k.hC = S.sb("hC", [128, KC, NT], BF16, ph12)
            with ExitStack() as ph:
                alloc_x(ph, "a")
                alloc_ffn(ph, "a")
                xf, xb = x_fn(k)
                for which in range(2):
                    load_x(k, xC_d if which == 0 else xH_d)
                    if DBG.get('p1') == 'ls':
                        continue
                    rmsnorm_fm(k, xf, xb, G_FFN1, *h_fn(k.hA), NT)
                    if DBG.get('p1') == 'n1':
                        continue
                    ffn(k, W["ffn1_w_gate"], W["ffn1_w_up"], W["ffn1_w_down"])
                    tgt = k.hC if which == 0 else k.hA
                    rmsnorm_fm(k, xf, xb, G_MIX, *h_fn(tgt), NT)
                store_x(x1_d, x1b)
                if stage == 1:
                    store_x(out_d, outb)
                finish()
            if stage == 1:
                return nc

            with ExitStack() as ph:
                k.oT = S.sb("oT", [128, 16, NT], BF16, ph)
                k.qT = S.sb("qT", [128, 1024], BF16, ph)
                Pbase = [S.sb("P%d" % i, [128, 256], BF16, ph) for i in range(8)]
                with ExitStack() as pa:
                    k.cos2 = S.sb("cos2", [128, 2048], F32, pa)
                    k.sin2 = S.sb("sin2", [128, 2048], F32, pa)
                    S.dma("sp", k.cos2[:, :], cos_d, writes=[k.cos2])
                    S.dma("sp", k.sin2[:, :], sin_d, writes=[k.sin2])
                    k.kT = S.sb("kT", [128, 2048], BF16, pa)
                    k.Pring = Rot(Pbase + [S.sb("Px%d" % i, [128, 256], BF16, pa) for i in range(2)])
                    k.V1 = S.sb("V1", [128, 9, 128], BF16, pa)
                    k.V4 = S.sb("V4", [128, 12, 128], BF16, pa)
                    k.V16 = S.sb("V16", [128, 16, 128], BF16, pa)
                    k.OD = S.sb("OD", [128, 2, 1024], F32, pa)
                    k.vT = S.sb("vT", [128, 2048], F32, pa)
                    k.Ering = Rot([S.sb("E%d" % i, [128, 256], BF16, pa) for i in range(4)])
                    for h in range(8):
                        attn_head(k, h, W["w_in"], pa)
                    finish()
                pg = ph
                k.Pring = Rot(Pbase)
                k.glrT = S.sb("glrT", [32, 2048], BF16, pg)
                k.kf = S.sb("kf", [128, 2048], F32, pg)
                k.lbuf = S.sb("lbuf", [128, 2048], F32, pg)
                k.cl = S.sb("cl", [128, 2048], F32, pg)
                k.rmask = S.sb("rmask", [128, TT], F32, pg)
                k.keT = S.sb("keT", [128, 1024], BF16, pg)
                k.dtot = S.sb("dtot", [128, 32], F32, pg)
                k.og = S.sb("og", [128, 2, 1024], F32, pg)
                k.Sfr = Rot([S.sb("Sf%d" % i, [128, 256], F32, pg) for i in range(4)])
                k.Sbr = Rot([S.sb("Sb%d" % i, [128, 256], BF16, pg) for i in range(4)])
                k.vtring = Rot([S.sb("vt%d" % i, [128, 256], BF16, pg) for i in range(3)])
                k.kdring = Rot([S.sb("kd%d" % i, [128, 128], BF16, pg) for i in range(3)])
                k.stage = k.rsring
                S.op("dve", lambda e: e.memset(k.rmask[:, :], 1.0), [], [k.rmask])
                S.op("dve", lambda e: e.memset(k.rmask[:, 0:TT:64], 0.0), [k.rmask], [k.rmask])
                gla_prep(k, W["w_in"])
                for g in range(4):
                    gla_head(k, g, W["w_in"])
                if stage == 2:
                    ob = out_d.rearrange("(kc p) n -> p kc n", p=128)
                    for c in range(16):
                        xs = k.stage.next()
                        for tt in range(2):
                            CP(S, xs[:, :], k.oT[:, c, tt * TT:(tt + 1) * TT], [k.oT], [xs])
                            S.dma("sp", ob[:, c, tt * TT:(tt + 1) * TT], xs[:, :], reads=[xs], writes=[outb])
                    finish()
                    return nc
                wout_to_dram(k, W["w_out"], k.oT, x1_d, x1b, x2_d, x2b)
                finish()

        with ExitStack() as ph3:
            k.kx = S.sb("kx", [128, 4, 256], BF16, ph3)
            k.vx = S.sb("vx", [128, 2, 512], BF16, ph3)
            alloc_x(ph3, "b")
            xf, xb = x_fn(k)
            with ExitStack() as ph:
                k.memf = S.sb("memf", [128, KC, 256], F32, ph)
                k.mT = S.sb("mT", [128, KC, 256], BF16, ph)
                k.kf = S.sb("kfx", [128, 256], F32, ph)
                k.kn = S.sb("knx", [128, 256], F32, ph)
                k.qf = S.sb("qfx", [128, 1024], F32, ph)
                k.qn = S.sb("qnx", [128, 1024], F32, ph)
                k.qT = S.sb("qTx", [128, 1024], BF16, ph)
                k.Ering = Rot([S.sb("Ex%d" % i, [128, 512], BF16, ph) for i in range(4)])
                k.oT = S.sb("oTx", [128, 4, NT], BF16, ph)
                S.dma("sp", k.memf[:, :, :], memT_d.rearrange("(kc p) n -> p kc n", p=128), writes=[k.memf])
                load_x(k, x2_d)
                xattn_mem(k, W["xattn_w_k"], W["xattn_w_v"], memT_d)
                rmsnorm_fm(k, xf, xb, G_XA, *h_fn(k.hA), NT)
                xattn(k, W["xattn_w_q"], W["xattn_w_o"])
                if stage == 4:
                    store_x(out_d, outb)
                    finish()
                    return nc
                finish()
            with ExitStack() as ph:
                alloc_ffn(ph, "b")
                rmsnorm_fm(k, xf, xb, G_FFN2, *h_fn(k.hA), NT)
                ffn(k, W["ffn2_w_gate"], W["ffn2_w_up"], W["ffn2_w_down"])
                store_x(out_d, outb)
                finish()
    return nc


def _fm(a):
    return np.ascontiguousarray(np.asarray(a, dtype=np.float32).T)


def _col(v):
    v = np.asarray(v, dtype=np.float32).reshape(-1, 128)
    return np.ascontiguousarray(v.T)


def _consts(half):
    p = np.arange(128)
    inv = (10000.0 ** (-(np.arange(64, dtype=np.float32)) / 64.0)).astype(np.float32)
    pos = np.arange(2048, dtype=np.float32) - (0.0 if half == 1 else 1024.0)
    ang = pos[None, :].astype(np.float32) * inv[p % 64][:, None]
    cos2 = np.cos(ang).astype(np.float32)
    sin2 = np.sin(ang).astype(np.float32)
    sin2[:64] *= -1.0
    kk = p[:, None]
    qq = p[None, :]
    valid = 1.0 if half == 1 else 0.0
    m_same = (kk <= qq).astype(np.float32)
    m_next = (kk >= qq).astype(np.float32)
    m_nextc = m_next * valid
    q16 = 64 + np.arange(64)[None, :]
    m16 = np.where(kk < 64, valid, (kk <= q16).astype(np.float32)).astype(np.float32)
    m16 = np.broadcast_to(m16, (128, 64))
    pad = np.zeros((128, 64), np.float32)
    mg = ((kk // 64 == qq // 64) & (kk <= qq)).astype(np.float32)
    masks = np.concatenate([m_same, m_next, m_nextc, m16, pad, mg], axis=1).astype(np.float32)
    ident = np.eye(128, dtype=np.float32)
    perm = np.zeros((128, 128), np.float32)
    perm[(p + 64) % 128, p] = 1.0
    mats = np.concatenate([ident, perm], axis=1)
    return cos2, sin2, np.ascontiguousarray(masks), np.ascontiguousarray(mats)


_PROG = {}


def kernel(x, mem, ffn1_norm, ffn1_w_gate, ffn1_w_up, ffn1_w_down, mix_norm, w_in,
           attn_q_norm, attn_k_norm, gla_w_gate2, gla_b_gate2, gla_out_norm, w_out,
           xattn_norm, mem_norm, xattn_w_q, xattn_w_k, xattn_w_v, xattn_q_norm, xattn_k_norm,
           xattn_w_o, ffn2_norm, ffn2_w_gate, ffn2_w_up, ffn2_w_down, _stage=99, _cores=None):
    x = np.asarray(x, np.float32)
    mem = np.asarray(mem, np.float32)
    f = lambda a: np.ascontiguousarray(np.asarray(a, np.float32)[0])
    gains = np.zeros((128, NG), np.float32)
    gains[:, G_FFN1:G_FFN1 + 16] = _col(f(ffn1_norm))
    gains[:, G_MIX:G_MIX + 16] = _col(f(mix_norm))
    gains[:, G_XA:G_XA + 16] = _col(f(xattn_norm))
    gains[:, G_MEM:G_MEM + 16] = _col(f(mem_norm))
    gains[:, G_FFN2:G_FFN2 + 16] = _col(f(ffn2_norm))
    gains[:, G_AQ] = f(attn_q_norm)
    gains[:, G_AK] = f(attn_k_norm)
    gains[:, G_XQ] = f(xattn_q_norm)
    gains[:, G_XK] = f(xattn_k_norm)
    gains[:, G_GO:G_GO + 2] = _col(f(gla_out_norm))
    w2aug = np.ascontiguousarray(np.concatenate([f(gla_w_gate2), f(gla_b_gate2)[None, :]], axis=0))
    common = {
        "ffn1_w_gate": f(ffn1_w_gate), "ffn1_w_up": f(ffn1_w_up), "ffn1_w_down": f(ffn1_w_down), "w_in": f(w_in),
        "w_out": f(w_out), "xattn_w_q": f(xattn_w_q), "xattn_w_k": f(xattn_w_k), "xattn_w_v": f(xattn_w_v),
        "xattn_w_o": f(xattn_w_o), "ffn2_w_gate": f(ffn2_w_gate), "ffn2_w_up": f(ffn2_w_up), "ffn2_w_down": f(ffn2_w_down),
        "gains": gains, "w2aug": w2aug,
    }
    B = x.shape[0]
    jobs = [(b, s) for b in range(B) for s in range(2)]
    if _cores is not None:
        jobs = jobs[:_cores]
    consts = {s: _consts(s) for s in (0, 1)}
    in_maps = []
    for (b, s) in jobs:
        cos2, sin2, masks, mats = consts[s]
        xh = _fm(x[b, s * NT:(s + 1) * NT])
        xc = _fm(x[b, 0:NT]) if s == 1 else np.zeros((D, NT), np.float32)
        m = dict(common)
        m.update({"xC": xc, "xH": xh, "memT": _fm(mem[b]), "cos2": cos2, "sin2": sin2, "masks": masks, "mats": mats})
        in_maps.append(m)
    key = _stage
    if key not in _PROG:
        _PROG[key] = build_program(stage=_stage)
    res = run_bass_kernel_spmd(_PROG[key], in_maps, core_ids=list(range(len(jobs))))
    out = np.zeros((B, 2 * NT, D), np.float32)
    for (b, s), r in zip(jobs, res.results):
        out[b, s * NT:(s + 1) * NT] = np.asarray(r["outT"], np.float32).T
    return out
```

```python
import numpy as np
from contextlib import ExitStack
import concourse.bass as bass
import concourse.mybir as mybir
from concourse.bass_utils import run_bass_kernel_spmd

F32 = mybir.dt.float32
BF16 = mybir.dt.bfloat16
AF = mybir.ActivationFunctionType
ALU = mybir.AluOpType

D = 2048
KC = 16
NT = 1024
TT = 512
DFF = 5632
EPS = 1e-6
ENGS = ("pe", "act", "dve", "pool", "sp")


class Buf:
    __slots__ = ("name", "t", "w_ev", "r_evs", "dsem", "dcount")

    def __init__(self, name, t):
        self.name = name
        self.t = t
        self.w_ev = None
        self.r_evs = []
        self.dsem = None
        self.dcount = 0

    def __getitem__(self, idx):
        return self.t[idx]


class Sched:
    def __init__(self, nc, stack):
        self.nc = nc
        self.stack = stack
        self.streams = {e: [] for e in ENGS}
        self.count = {e: 0 for e in ENGS}
        self.seen = {e: {} for e in ENGS}
        self.sems = {}
        for e in ENGS:
            self.sems[e] = stack.enter_context(nc.semaphore("s_" + e))
        self.nd = 0
        self.pending_sig = {e: False for e in ENGS}
        self.latest = {}

    def sb(self, name, shape, dtype, stack=None):
        t = (stack or self.stack).enter_context(self.nc.sbuf_tensor("sb_" + name, list(shape), dtype))
        return Buf(name, t)

    def ps(self, name, shape, dtype=F32):
        t = self.stack.enter_context(self.nc.psum_tensor("pp_" + name, list(shape), dtype))
        return Buf(name, t)

    def dram(self, name, ap):
        return Buf(name, ap)

    def _dsem(self, b):
        if b.dsem is None:
            self.nd += 1
            key = "d%d" % self.nd
            self.sems[key] = self.stack.enter_context(self.nc.semaphore(key))
            b.dsem = key
        return b.dsem

    def _need(self, eng, ev):
        if ev is None:
            return
        key, val = ev
        if key == eng and (eng == "pe" or val > self.count[eng]):
            return
        if self.seen[eng].get(key, 0) >= val:
            return
        self.seen[eng][key] = val
        self.streams[eng].append(("wait", key, val))

    def _deps(self, eng, reads, writes):
        for b in reads:
            self._need(eng, b.w_ev)
        for b in writes:
            self._need(eng, b.w_ev)
            for ev in b.r_evs:
                self._need(eng, ev)

    def _record(self, ev, reads, writes):
        self.latest[ev[0]] = max(self.latest.get(ev[0], 0), ev[1])
        for b in reads:
            if not b.r_evs or b.r_evs[-1] != ev:
                b.r_evs.append(ev)
        for b in writes:
            b.w_ev = ev
            b.r_evs = []

    def op(self, eng, fn, reads=(), writes=(), sig=True):
        self._deps(eng, reads, writes)
        if sig:
            self.count[eng] += 1
            ev = (eng, self.count[eng])
            self.streams[eng].append(("op", fn, eng, 1))
            self.pending_sig[eng] = False
        else:
            ev = (eng, self.count[eng] + 1)
            self.streams[eng].append(("op", fn, None, 0))
            self.pending_sig[eng] = True
        self._record(ev, reads, writes)
        return ev

    def dmaop(self, q, fn, reads=(), writes=()):
        assert len(writes) == 1
        self._deps(q, reads, writes)
        wb = writes[0]
        key = self._dsem(wb)
        wb.dcount += 16
        ev = (key, wb.dcount)
        self.streams[q].append(("op", fn, key, 16))
        self._record(ev, reads, writes)
        return ev

    def dma(self, q, out_ap, in_ap, reads=(), writes=()):
        return self.dmaop(q, lambda e: e.dma_start(out=out_ap, in_=in_ap), reads, writes)

    def barrier(self):
        for e in ENGS:
            assert not self.pending_sig[e]
        for e in ENGS:
            for key, val in self.latest.items():
                self._need(e, (key, val))

    def emit(self):
        nc = self.nc
        for e in ENGS:
            assert not self.pending_sig[e], "engine %s ends with unsignalled op" % e
        streams = self.streams
        self.streams = {e: [] for e in ENGS}
        if DBG.get('sim'):
            vals = self.__dict__.setdefault('simvals', {})
            pc = {e: 0 for e in ENGS}
            prog = True
            while prog:
                prog = False
                for e in ENGS:
                    st_ = streams[e]
                    while pc[e] < len(st_):
                        it = st_[pc[e]]
                        if it[0] == "wait":
                            if vals.get(it[1], 0) >= it[2]:
                                pc[e] += 1
                                prog = True
                            else:
                                break
                        else:
                            if it[2] is not None:
                                vals[it[2]] = vals.get(it[2], 0) + it[3]
                            pc[e] += 1
                            prog = True
            for e in ENGS:
                if pc[e] < len(streams[e]):
                    it = streams[e][pc[e]]
                    print("SIM DEADLOCK: engine", e, "stuck at", pc[e], "/", len(streams[e]), it[:3], "have", vals.get(it[1], 0))
            print("SIM block done", {e: len(streams[e]) for e in ENGS})
            return
        with nc.Block() as block:
            def run(engname):
                def body(eng):
                    for item in streams[engname]:
                        if item[0] == "wait":
                            eng.wait_ge(self.sems[item[1]], item[2])
                        else:
                            _, fn, key, inc = item
                            ins = fn(eng)
                            if key is not None:
                                ins.then_inc(self.sems[key], inc)
                return body
            block.tensor(run("pe"))
            block.scalar(run("act"))
            block.vector(run("dve"))
            block.gpsimd(run("pool"))
            block.sync(run("sp"))


DBG = {}


class K:
    pass


def MM(S, out_ap, lhsT, rhs, start, stop, reads, writes, sig=None):
    if sig is None:
        sig = stop
    S.op("pe", lambda e: e.matmul(out_ap, lhsT=lhsT, rhs=rhs, start=start, stop=stop), reads, writes, sig)


def ACT(S, out_ap, in_ap, func, reads, writes, bias=None, scale=None):
    kw = {}
    if bias is not None:
        kw["bias"] = bias
    if scale is not None:
        kw["scale"] = scale
    S.op("act", lambda e: e.activation(out=out_ap, in_=in_ap, func=func, **kw), reads, writes)


def TT_(S, out_ap, in0, in1, op, reads, writes, eng="dve"):
    S.op(eng, lambda e: e.tensor_tensor(out=out_ap, in0=in0, in1=in1, op=op), reads, writes)


def TS(S, out_ap, in0, s1, s2, op0, op1, reads, writes, eng="dve"):
    if s2 is None:
        S.op(eng, lambda e: e.tensor_scalar(out=out_ap, in0=in0, scalar1=s1, scalar2=None, op0=op0), reads, writes)
    else:
        S.op(eng, lambda e: e.tensor_scalar(out=out_ap, in0=in0, scalar1=s1, scalar2=s2, op0=op0, op1=op1), reads, writes)


def STT(S, out_ap, in0, scalar, in1, op0, op1, reads, writes, eng="dve"):
    S.op(eng, lambda e: e.scalar_tensor_tensor(out=out_ap, in0=in0, scalar=scalar, in1=in1, op0=op0, op1=op1), reads, writes)


def CP(S, out_ap, in_ap, reads, writes, eng="dve"):
    if eng == "act":
        S.op("act", lambda e: e.copy(out=out_ap, in_=in_ap), reads, writes)
    else:
        S.op(eng, lambda e: e.tensor_copy(out=out_ap, in_=in_ap), reads, writes)


class Rot:
    def __init__(self, items):
        self.items = items
        self.i = 0

    def next(self):
        b = self.items[self.i % len(self.items)]
        self.i += 1
        return b


def wload(k, view_fn, dram_ap):
    slot = k.wring.next()
    k.S.dma("pool", view_fn(slot), dram_ap, writes=[slot])
    return slot


def v3(n1, n2):
    return lambda slot: slot[:, 0:n1 * n2].rearrange("p (a b) -> p a b", a=n1)


def rstd_from_ps(k, ps, n, width, rs):
    S = k.S
    ACT(S, rs[:, 0:width], ps[:, 0:width], AF.Ln, [ps, k.epscol], [rs], bias=k.epscol[:, 0:1], scale=1.0 / n)
    ACT(S, rs[:, 0:width], rs[:, 0:width], AF.Exp, [rs], [rs], scale=-0.5)


def recip_act(k, out_ap, in_ap, reads, writes):
    S = k.S
    ACT(S, out_ap, in_ap, AF.Ln, reads, writes)
    ACT(S, out_ap, out_ap, AF.Exp, writes, writes, scale=-1.0)


def rmsnorm_fm(k, x_ap_fn, xbufs, gcol, out_ap_fn, outbufs, ntok):
    S = k.S
    for lo in range(0, ntok, TT):
        hi = min(lo + TT, ntok)
        w = hi - lo
        ps = k.psum.next()
        for kc in range(KC):
            sq = k.sqring.next()
            ACT(S, sq[:, 0:w], x_ap_fn(kc, lo, hi), AF.Square, xbufs(kc, lo), [sq])
            MM(S, ps[:, 0:w], k.ones[:, :], sq[:, 0:w], kc == 0, kc == KC - 1, [sq, k.ones], [ps], sig=True)
        rs = k.rsring.next()
        rstd_from_ps(k, ps, float(D), w, rs)
        for kc in range(KC):
            STT(S, out_ap_fn(kc, lo, hi), x_ap_fn(kc, lo, hi), k.gains[:, gcol + kc:gcol + kc + 1], rs[:, 0:w],
                ALU.mult, ALU.mult, xbufs(kc, lo) + [rs, k.gains], outbufs(kc, lo))


def ffn(k, wg, wu, wd):
    S = k.S
    xT, hA = k.xT, k.hA
    nblk = DBG.get('nblk', DFF // 512)
    for blk in range(nblk):
        c0 = blk * 512
        gs = [wload(k, v3(16, 256), wg[:, c0 + i * 256:c0 + (i + 1) * 256].rearrange("(kc p) c -> p kc c", p=128)) for i in range(2)]
        us = [wload(k, v3(16, 256), wu[:, c0 + i * 256:c0 + (i + 1) * 256].rearrange("(kc p) c -> p kc c", p=128)) for i in range(2)]
        aT = k.aT.next()
        for mc in range(4):
            gsl = v3(16, 256)(gs[mc // 2])
            usl = v3(16, 256)(us[mc // 2])
            off = (mc % 2) * 128
            psg = [k.psum.next() for _ in range(2)]
            for kc in range(KC):
                for tt in range(2):
                    MM(S, psg[tt][:, :], gsl[:, kc, off:off + 128], hA[:, kc, tt * TT:(tt + 1) * TT], kc == 0, kc == KC - 1,
                       [gs[mc // 2], hA], [psg[tt]])
            psu = [k.psum.next() for _ in range(2)]
            for kc in range(KC):
                for tt in range(2):
                    MM(S, psu[tt][:, :], usl[:, kc, off:off + 128], hA[:, kc, tt * TT:(tt + 1) * TT], kc == 0, kc == KC - 1,
                       [us[mc // 2], hA], [psu[tt]])
            for tt in range(2):
                sg = k.sgring.next()
                ACT(S, sg[:, :], psg[tt][:, :], AF.Silu, [psg[tt]], [sg])
                TT_(S, aT[:, mc, tt * TT:(tt + 1) * TT], sg[:, :], psu[tt][:, :], ALU.mult, [sg, psu[tt]], [aT])
        ds = [wload(k, v3(4, 1024), wd[c0:c0 + 512, i * 1024:(i + 1) * 1024].rearrange("(kc p) c -> p kc c", p=128)) for i in range(2)]
        for m in range(KC):
            dsl = v3(4, 1024)(ds[m // 8])
            off = (m % 8) * 128
            pss = [k.psum.next() for _ in range(2)]
            for kc in range(4):
                for tt in range(2):
                    MM(S, pss[tt][:, :], dsl[:, kc, off:off + 128], aT[:, kc, tt * TT:(tt + 1) * TT], kc == 0, kc == 3,
                       [ds[m // 8], aT], [pss[tt]])
            for tt in range(2):
                xb = xT[m][tt]
                STT(S, xb[:, :], pss[tt][:, :], 0.5, xb[:, :], ALU.mult, ALU.add, [pss[tt], xb], [xb])


def x_fn(k):
    return (lambda kc, lo, hi: k.xT[kc][lo // TT][:, 0:hi - lo]), (lambda kc, lo: [k.xT[kc][lo // TT]])


def h_fn(hbuf):
    return (lambda kc, lo, hi: hbuf[:, kc, lo:hi]), (lambda kc, lo: [hbuf])


def load_x(k, x_dram):
    S = k.S
    xv = x_dram.rearrange("(kc p) n -> p kc n", p=128)
    for kc in range(KC):
        for tt in range(2):
            S.dma("sp", k.xT[kc][tt][:, :], xv[:, kc, tt * TT:(tt + 1) * TT], writes=[k.xT[kc][tt]])


def headnorm(k, src, width, gcol, dst_f32):
    S = k.S
    for lo in range(0, width, TT):
        w = min(TT, width - lo)
        sq = k.sqring.next()
        ACT(S, sq[:, 0:w], src[:, lo:lo + w], AF.Square, [src], [sq])
        ps = k.psum.next()
        MM(S, ps[:, 0:w], k.ones[:, :], sq[:, 0:w], True, True, [sq, k.ones], [ps])
        rs = k.rsring.next()
        rstd_from_ps(k, ps, 128.0, w, rs)
        STT(S, dst_f32[:, lo:lo + w], src[:, lo:lo + w], k.gains[:, gcol:gcol + 1], rs[:, 0:w], ALU.mult, ALU.mult,
            [src, rs, k.gains], [dst_f32])


def rope(k, xn, width, tok0, dst_bf):
    S = k.S
    for lo in range(0, width, TT):
        w = min(TT, width - lo)
        ps = k.psum.next()
        MM(S, ps[:, 0:w], k.perm, xn[:, lo:lo + w], True, True, [xn, k.mats], [ps])
        t1 = k.rsring.next()
        TT_(S, t1[:, 0:w], xn[:, lo:lo + w], k.cos2[:, tok0 + lo:tok0 + lo + w], ALU.mult, [xn, k.cos2], [t1])
        t2 = k.rsring.next()
        TT_(S, t2[:, 0:w], ps[:, 0:w], k.sin2[:, tok0 + lo:tok0 + lo + w], ALU.mult, [ps, k.sin2], [t2])
        TT_(S, dst_bf[:, lo:lo + w], t1[:, 0:w], t2[:, 0:w], ALU.add, [t1, t2], [dst_bf])


def proj_fm(k, wslot, wsl, col0, srcs, dst_f32):
    S = k.S
    o = 0
    for hbuf, ntok in srcs:
        for lo in range(0, ntok, TT):
            ps = k.psum.next()
            for kc in range(KC):
                MM(S, ps[:, :], wsl[:, kc, col0:col0 + 128], hbuf[:, kc, lo:lo + TT], kc == 0, kc == KC - 1, [wslot, hbuf], [ps])
            CP(S, dst_f32[:, o:o + TT], ps[:, :], [ps], [dst_f32], eng="act")
            o += TT


def qk_pipeline(k, tiles, fillers=None):
    S = k.S
    st = [dict() for _ in tiles]

    def stageA(i):
        wslot, wsl, col0, hbuf, lo, gcol, tok0, dst_bf, dst_lo = tiles[i]
        ps = k.psum.next()
        for kc in range(KC):
            MM(S, ps[:, :], wsl[:, kc, col0:col0 + 128], hbuf[:, kc, lo:lo + TT], kc == 0, kc == KC - 1, [wslot, hbuf], [ps])
        xf = k.rsring.next()
        CP(S, xf[:, :], ps[:, :], [ps], [xf], eng="act")
        sq = k.sqring.next()
        ACT(S, sq[:, :], xf[:, :], AF.Square, [xf], [sq])
        st[i]["xf"], st[i]["sq"] = xf, sq

    def stageB(i):
        wslot, wsl, col0, hbuf, lo, gcol, tok0, dst_bf, dst_lo = tiles[i]
        xf, sq = st[i]["xf"], st[i]["sq"]
        ps2 = k.psum.next()
        MM(S, ps2[:, :], k.ones[:, :], sq[:, :], True, True, [sq, k.ones], [ps2])
        rs = k.rsring.next()
        rstd_from_ps(k, ps2, 128.0, TT, rs)
        xn = k.rsring.next()
        STT(S, xn[:, :], xf[:, :], k.gains[:, gcol:gcol + 1], rs[:, :], ALU.mult, ALU.mult, [xf, rs, k.gains], [xn])
        xnb = k.sqring.next()
        CP(S, xnb[:, :], xn[:, :], [xn], [xnb], eng="act")
        st[i]["xn"], st[i]["xnb"] = xn, xnb

    def stageC(i):
        wslot, wsl, col0, hbuf, lo, gcol, tok0, dst_bf, dst_lo = tiles[i]
        xn, xnb = st[i]["xn"], st[i]["xnb"]
        ps3 = k.psum.next()
        MM(S, ps3[:, :], k.permb[:, :], xnb[:, :], True, True, [xnb, k.permb], [ps3])
        t1 = k.rsring.next()
        TT_(S, t1[:, :], xn[:, :], k.cos2[:, tok0:tok0 + TT], ALU.mult, [xn, k.cos2], [t1], eng="pool")
        t2 = k.rsring.next()
        TT_(S, t2[:, :], ps3[:, :], k.sin2[:, tok0:tok0 + TT], ALU.mult, [ps3, k.sin2], [t2])
        TT_(S, dst_bf[:, dst_lo:dst_lo + TT], t1[:, :], t2[:, :], ALU.add, [t1, t2], [dst_bf], eng="pool")

    n = len(tiles)
    fillers = list(fillers or [])
    for step in range(n + 2):
        if step < n:
            stageA(step)
        elif fillers:
            fillers.pop(0)()
        if 0 <= step - 1 < n:
            stageB(step - 1)
        if 0 <= step - 2 < n:
            stageC(step - 2)
    while fillers:
        fillers.pop(0)()


def attn_wload(k, h, w_in):
    f = v3(16, 128)
    wv = wload(k, f, w_in[:, 2048 + h * 128:2048 + (h + 1) * 128].rearrange("(kc p) c -> p kc c", p=128))
    wk = wload(k, f, w_in[:, 1024 + h * 128:1024 + (h + 1) * 128].rearrange("(kc p) c -> p kc c", p=128))
    wq = wload(k, f, w_in[:, h * 128:(h + 1) * 128].rearrange("(kc p) c -> p kc c", p=128))
    return wq, wk, wv


def attn_head(k, h, w_in, st):
    S = k.S
    hC, hH = k.hC, k.hA
    if getattr(k, "attn_w", None) is None:
        k.attn_w = attn_wload(k, h, w_in)
    wq, wk, wv = k.attn_w
    k.attn_w = None
    wqs, wks, wvs = v3(16, 128)(wq), v3(16, 128)(wk), v3(16, 128)(wv)
    kT, qT = k.kT, k.qT
    tiles = []
    for i, (hb, lo) in enumerate(((hC, 0), (hC, TT), (hH, 0), (hH, TT))):
        tiles.append((wk, wks, 0, hb, lo, k.G_AK, i * TT, kT, i * TT))
    for i in range(2):
        tiles.append((wq, wqs, 0, hH, i * TT, k.G_AQ, 1024 + i * TT, qT, i * TT))
    def vgroup(hb, lo, o):
        def f():
            ps = k.psum.next()
            for kc in range(KC):
                MM(S, ps[:, :], wvs[:, kc, 0:128], hb[:, kc, lo:lo + TT], kc == 0, kc == KC - 1, [wv, hb], [ps])
            CP(S, k.vT[:, o:o + TT], ps[:, :], [ps], [k.vT], eng="act")
        return f
    vfill = [vgroup(hC, 0, 0), vgroup(hC, TT, TT), vgroup(hH, 0, 2 * TT), vgroup(hH, TT, 3 * TT)]
    vfill.pop(0)()
    vfill.pop(0)()
    qk_pipeline(k, tiles, vfill)
    if h < 7:
        k.attn_w = attn_wload(k, h + 1, w_in)

    V1, V4, V16, vT = k.V1, k.V4, k.V16, k.vT
    tjobs = []
    for j in range(7, 16):
        tjobs.append((V1, j - 7, 128 * j, 1))
    for c in range(4):
        for j in range(1, 4):
            tjobs.append((V4, c * 3 + j - 1, c + 4 * 128 * j, 4))
    for c in range(16):
        tjobs.append((V16, c, c, 16))
    for g0 in range(0, len(tjobs), 4):
        grp = tjobs[g0:g0 + 4]
        ps = k.psum.next()
        for i, (db, di, t0, stp) in enumerate(grp):
            S.op("pe", lambda e, ps=ps, i=i, t0=t0, stp=stp: e.transpose(ps[:, i * 128:(i + 1) * 128],
                                                                          vT[:, t0:t0 + stp * 127 + 1:stp], k.ident),
                 [vT, k.mats], [ps])
        db0, di0 = grp[0][0], grp[0][1]
        same = all(g[0] is db0 for g in grp) and all(grp[i][1] == di0 + i for i in range(len(grp)))
        if same:
            n = len(grp)
            CP(S, db0[:, di0:di0 + n, :], ps[:, 0:n * 128].rearrange("p (a b) -> p a b", a=n), [ps], [db0])
        else:
            for i, (db, di, t0, stp) in enumerate(grp):
                CP(S, db[:, di, :], ps[:, i * 128:(i + 1) * 128], [ps], [db])

    OD = k.OD
    sc = 128.0 ** -0.5
    M_same, M_next, M_nextC, M16 = k.M_same, k.M_next, k.M_nextC, k.M16

    def keys_ap(r, c, j, n=128, m_lo=0):
        start = c + r * (128 * j + m_lo)
        return kT[:, start:start + r * (n - 1) + 1:r]

    def q_ap(r, c, i, n=128, m_lo=0):
        start = c + r * (128 * i + m_lo) - 1024
        return qT[:, start:start + r * (n - 1) + 1:r]

    def acc_ap(r, c, i, n=128, m_lo=0):
        start = c + r * (128 * i + m_lo) - 1024
        return OD[:, :, start:start + r * (n - 1) + 1:r]

    sjobs = []
    pjobs = []
    s0 = len(sjobs)
    sjobs.append((1, 0, 7, [(8, M_nextC, 128, 0)]))
    for j in range(8, 16):
        parts = [(j, M_same, 128, 0)]
        if j < 15:
            parts.append((j + 1, M_next, 128, 0))
        sjobs.append((1, 0, j, parts))
        pjobs.append((1, 0, j, [(V1, V1[:, j - 1 - 7, :], s0 + j - 8, 0 if j == 8 else 128), (V1, V1[:, j - 7, :], s0 + j - 7, 0)], 128, 0, True))
    for c in range(4):
        s0 = len(sjobs)
        sjobs.append((4, c, 1, [(2, M_nextC, 128, 0)]))
        sjobs.append((4, c, 2, [(2, M_same, 128, 0), (3, M_next, 128, 0)]))
        sjobs.append((4, c, 3, [(3, M_same, 128, 0)]))
        pjobs.append((4, c, 2, [(V4, V4[:, c * 3 + 0, :], s0, 0), (V4, V4[:, c * 3 + 1, :], s0 + 1, 0)], 128, 0, False))
        pjobs.append((4, c, 3, [(V4, V4[:, c * 3 + 1, :], s0 + 1, 128), (V4, V4[:, c * 3 + 2, :], s0 + 2, 0)], 128, 0, False))
    for c in range(16):
        s0 = len(sjobs)
        sjobs.append((16, c, 0, [(0, M16, 64, 64)]))
        pjobs.append((16, c, 0, [(V16, V16[:, c, :], s0, 0)], 64, 64, False))

    Ps = {}
    state = {"next": 0}

    def emit_scores(upto):
        while state["next"] <= min(upto, len(sjobs) - 1):
            r, c, j, qparts = sjobs[state["next"]]
            ps = k.psum.next()
            P = k.Pring.next()
            o = 0
            for (i, mask, nq, m_lo) in qparts:
                MM(S, ps[:, o:o + nq], keys_ap(r, c, j), q_ap(r, c, i, nq, m_lo), True, True, [kT, qT], [ps])
                o += nq
            E = k.Ering.next()
            ACT(S, E[:, 0:o], ps[:, 0:o], AF.Exp, [ps], [E], scale=sc)
            o = 0
            for (i, mask, nq, m_lo) in qparts:
                TT_(S, P[:, o:o + nq], E[:, o:o + nq], mask, ALU.mult, [E, k.masks], [P], eng="pool")
                o += nq
            Ps[state["next"]] = P
            state["next"] += 1

    LOOK = 7
    for (r, c, i, contribs, nq, m_lo, first) in pjobs:
        emit_scores(max(ci[2] for ci in contribs) + LOOK)
        ps = k.psum.next()
        n = len(contribs)
        for idx, (vb, vap, sidx, po) in enumerate(contribs):
            P = Ps[sidx]
            MM(S, ps[:, 0:nq], vap, P[:, po:po + nq], idx == 0, idx == n - 1, [vb, P], [ps], sig=False)
        for idx, (vb, vap, sidx, po) in enumerate(contribs):
            P = Ps[sidx]
            MM(S, ps[:, 128:128 + nq], k.ones[:, :], P[:, po:po + nq], idx == 0, idx == n - 1, [k.ones, P], [ps], sig=(idx == n - 1))
        psv = ps[:, 0:256].rearrange("p (a b) -> p a b", a=2)[:, :, 0:nq]
        if first:
            CP(S, acc_ap(r, c, i, nq, m_lo), psv, [ps], [OD])
        else:
            TT_(S, acc_ap(r, c, i, nq, m_lo), acc_ap(r, c, i, nq, m_lo), psv, ALU.add, [ps, OD], [OD])
    recip_act(k, OD[:, 1, :], OD[:, 1, :], [OD], [OD])
    TT_(S, k.oT[:, h, :], OD[:, 0, :], OD[:, 1, :], ALU.mult, [OD], [k.oT])


def gla_prep(k, w_in):
    S = k.S
    wsl_fn = v3(16, 16)
    wg = wload(k, wsl_fn, w_in[:, 6144:6160].rearrange("(kc p) c -> p kc c", p=128))
    wsl = wsl_fn(wg)
    S.op("dve", lambda e: e.memset(k.glrT[:, :], 1.0), [], [k.glrT])
    o = 0
    for hb in (k.hC, k.hA):
        for lo in range(0, NT, TT):
            ps = k.psum.next()
            for kc in range(KC):
                MM(S, ps[0:16, :], wsl[:, kc, :], hb[:, kc, lo:lo + TT], kc == 0, kc == KC - 1, [wg, hb], [ps])
            CP(S, k.glrT[0:16, o:o + TT], ps[0:16, :], [ps], [k.glrT], eng="act")
            o += TT


def gla_head(k, g, w_in):
    S = k.S
    hC, hH = k.hC, k.hA
    QG0, KG0, VG0, RG0 = 3072, 3584, 4096, 5120
    f128 = v3(16, 128)
    f256 = v3(16, 256)
    wq = wload(k, f128, w_in[:, QG0 + g * 128:QG0 + (g + 1) * 128].rearrange("(kc p) c -> p kc c", p=128))
    wk = wload(k, f128, w_in[:, KG0 + g * 128:KG0 + (g + 1) * 128].rearrange("(kc p) c -> p kc c", p=128))
    wv = wload(k, f256, w_in[:, VG0 + g * 256:VG0 + (g + 1) * 256].rearrange("(kc p) c -> p kc c", p=128))
    kf, lbuf, cl = k.kf, k.lbuf, k.cl
    proj_fm(k, wk, f128(wk), 0, [(hC, NT), (hH, NT)], kf)
    for lo in range(0, 2048, TT):
        ps = k.psum.next()
        MM(S, ps[:, :], k.w2[0:17, g * 128:(g + 1) * 128], k.glrT[0:17, lo:lo + TT], True, True, [k.w2, k.glrT], [ps])
        e1 = k.rsring.next()
        ACT(S, e1[:, :], ps[:, :], AF.Exp, [ps], [e1], scale=-1.0)
        ACT(S, lbuf[:, lo:lo + TT], e1[:, :], AF.Ln, [e1], [lbuf], bias=k.onecol[:, 0:1])
        S.op("dve", lambda e, lo=lo: e.tensor_tensor_scan(out=cl[:, lo:lo + TT], data0=k.rmask[:, :], data1=lbuf[:, lo:lo + TT],
                                                           initial=0.0, op0=ALU.mult, op1=ALU.add), [k.rmask, lbuf], [cl])
    for i in range(2):
        lo = i * TT
        ps = k.psum.next()
        for kc in range(KC):
            MM(S, ps[:, :], f128(wq)[:, kc, :], hH[:, kc, lo:lo + TT], kc == 0, kc == KC - 1, [wq, hH], [ps])
        eb = k.rsring.next()
        ACT(S, eb[:, :], cl[:, 1024 + lo:1024 + lo + TT], AF.Exp, [cl], [eb], scale=-1.0 / 16)
        STT(S, k.qT[:, lo:lo + TT], ps[:, :], 128.0 ** -0.5, eb[:, :], ALU.mult, ALU.mult, [ps, eb], [k.qT])
        enb = k.rsring.next()
        ACT(S, enb[:, :], cl[:, 1024 + lo:1024 + lo + TT], AF.Exp, [cl], [enb], scale=1.0 / 16)
        TT_(S, k.keT[:, lo:lo + TT], kf[:, 1024 + lo:1024 + lo + TT], enb[:, :], ALU.mult, [kf, enb], [k.keT])
    dtot = k.dtot
    ACT(S, dtot[:, :], cl[:, 63:2048:64], AF.Exp, [cl], [dtot], scale=-1.0 / 16)
    dl = lbuf
    clv = cl[:, :].rearrange("p (n c) -> p n c", c=64)
    S.op("dve", lambda e: e.tensor_tensor(out=dl[:, :].rearrange("p (n c) -> p n c", c=64), in0=clv[:, :, 63:64].to_broadcast([128, 32, 64]),
                                          in1=clv, op=ALU.subtract), [cl], [dl])
    ACT(S, dl[:, :], dl[:, :], AF.Exp, [dl], [dl], scale=-1.0 / 16)
    TT_(S, kf[:, :], kf[:, :], dl[:, :], ALU.mult, [kf, dl], [kf])
    kdecT = kf
    Sfr, Sbr = k.Sfr, k.Sbr
    Sf = Sfr.next()
    S.op("dve", lambda e, Sf0=Sf: e.memset(Sf0[:, :], 0.0), [], [Sf])
    Sb = None
    og = k.og
    for blk in range(16):
        hb, lo = (hC, blk * 128) if blk < 8 else (hH, (blk - 8) * 128)
        ps = k.psum.next()
        for kc in range(KC):
            MM(S, ps[:, 0:256], hb[:, kc, lo:lo + 128], f256(wv)[:, kc, :], kc == 0, kc == KC - 1, [wv, hb], [ps])
        vt = k.vtring.next()
        CP(S, vt[:, :], ps[:, 0:256], [ps], [vt], eng="act")
        pst = k.psum.next()
        S.op("pe", lambda e, pst=pst, blk=blk: e.transpose(pst[:, 0:128], kdecT[:, blk * 128:(blk + 1) * 128], k.ident),
             [kdecT, k.mats], [pst])
        kd = k.kdring.next()
        CP(S, kd[:, :], pst[:, 0:128], [pst], [kd])
        own = blk >= 8
        Sb_before = [Sb, None]
        for ch in range(2):
            n = blk * 2 + ch
            p0 = ch * 64
            psd = k.psum.next()
            MM(S, psd[:, 0:256], kd[p0:p0 + 64, :], vt[p0:p0 + 64, :], True, True, [kd, vt], [psd])
            Sn = Sfr.next()
            STT(S, Sn[:, :], Sf[:, :], dtot[:, n:n + 1], psd[:, 0:256], ALU.mult, ALU.add, [Sf, dtot, psd], [Sn])
            Sf = Sn
            if n >= 15 and n < 31:
                Sb = Sbr.next()
                CP(S, Sb[:, :], Sf[:, :], [Sf], [Sb], eng="act")
            if ch == 0:
                Sb_before[1] = Sb
        if own:
            t0 = (blk - 8) * 128
            psa = k.psum.next()
            MM(S, psa[:, 0:128], k.keT[:, t0:t0 + 128], k.qT[:, t0:t0 + 128], True, True, [k.keT, k.qT], [psa])
            At = k.Pring.next()
            TT_(S, At[:, 0:128], psa[:, 0:128], k.MG, ALU.mult, [psa, k.masks], [At])
            pso = [k.psum.next() for _ in range(2)]
            for ch in range(2):
                p0 = ch * 64
                tq = (blk - 8) * 128 + ch * 64
                Sbb = Sb_before[ch]
                for dc in range(2):
                    MM(S, pso[dc][:, p0:p0 + 64], vt[:, dc * 128:(dc + 1) * 128], At[:, p0:p0 + 64], True, False, [vt, At], [pso[dc]], sig=False)
                    MM(S, pso[dc][:, p0:p0 + 64], Sbb[:, dc * 128:(dc + 1) * 128], k.qT[:, tq:tq + 64], False, True, [Sbb, k.qT], [pso[dc]],
                       sig=True)
            for dc in range(2):
                CP(S, og[:, dc, t0:t0 + 128], pso[dc][:, 0:128], [pso[dc]], [og], eng="act")
    wr = wload(k, f256, w_in[:, RG0 + g * 256:RG0 + (g + 1) * 256].rearrange("(kc p) c -> p kc c", p=128))
    for lo in range(0, NT, TT):
        ps = k.psum.next()
        for dc in range(2):
            sq = k.sqring.next()
            ACT(S, sq[:, :], og[:, dc, lo:lo + TT], AF.Square, [og], [sq])
            MM(S, ps[:, :], k.ones[:, :], sq[:, :], dc == 0, dc == 1, [sq, k.ones], [ps], sig=True)
        rs = k.rsring.next()
        rstd_from_ps(k, ps, 256.0, TT, rs)
        for dc in range(2):
            psr = k.psum.next()
            for kc in range(KC):
                MM(S, psr[:, :], f256(wr)[:, kc, dc * 128:(dc + 1) * 128], hH[:, kc, lo:lo + TT], kc == 0, kc == KC - 1, [wr, hH], [psr])
            sg = k.rsring.next()
            ACT(S, sg[:, :], psr[:, :], AF.Silu, [psr], [sg])
            t1 = k.rsring.next()
            STT(S, t1[:, :], og[:, dc, lo:lo + TT], k.gains[:, k.G_GO + dc:k.G_GO + dc + 1], rs[:, :], ALU.mult, ALU.mult,
                [og, rs, k.gains], [t1])
            TT_(S, k.oT[:, 8 + 2 * g + dc, lo:lo + TT], t1[:, :], sg[:, :], ALU.mult, [t1, sg], [k.oT])


def add_proj(k, w, nkc, src_bf):
    S = k.S
    for mp in range(8):
        fn = v3(nkc, 256)
        sl = wload(k, fn, w[:, mp * 256:(mp + 1) * 256].rearrange("(kc p) c -> p kc c", p=128))
        for mi in range(2):
            m = mp * 2 + mi
            pss = [k.psum.next() for _ in range(2)]
            for kc in range(nkc):
                for tt in range(2):
                    MM(S, pss[tt][:, :], fn(sl)[:, kc, mi * 128:(mi + 1) * 128], src_bf[:, kc, tt * TT:(tt + 1) * TT], kc == 0, kc == nkc - 1,
                       [sl, src_bf], [pss[tt]])
            for tt in range(2):
                xb = k.xT[m][tt]
                TT_(S, xb[:, :], xb[:, :], pss[tt][:, :], ALU.add, [pss[tt], xb], [xb])


def xattn_mem(k, wk, wv, memT_d):
    S = k.S
    memf, mT = k.memf, k.mT
    rmsnorm_fm(k, lambda kc, lo, hi: memf[:, kc, lo:hi], lambda kc, lo: [memf], k.G_MEM,
               lambda kc, lo, hi: mT[:, kc, lo:hi], lambda kc, lo: [mT], 256)
    f256 = v3(16, 256)
    kx = k.kx
    for hp in range(2):
        sl = wload(k, f256, wk[:, hp * 256:(hp + 1) * 256].rearrange("(kc p) c -> p kc c", p=128))
        for hi_ in range(2):
            hx = hp * 2 + hi_
            ps = k.psum.next()
            for kc in range(KC):
                MM(S, ps[:, 0:256], f256(sl)[:, kc, hi_ * 128:(hi_ + 1) * 128], mT[:, kc, :], kc == 0, kc == KC - 1, [sl, mT], [ps])
            CP(S, k.kf[:, 0:256], ps[:, 0:256], [ps], [k.kf], eng="act")
            headnorm(k, k.kf, 256, k.G_XK, k.kn)
            CP(S, kx[:, hx, :], k.kn[:, 0:256], [k.kn], [kx])
    vx = k.vx
    sls = [wload(k, f256, wv[:, i * 256:(i + 1) * 256].rearrange("(kc p) c -> p kc c", p=128)) for i in range(2)]
    for blk in range(2):
        ps = k.psum.next()
        for i in range(2):
            for kc in range(KC):
                MM(S, ps[:, i * 256:(i + 1) * 256], mT[:, kc, blk * 128:(blk + 1) * 128], f256(sls[i])[:, kc, :], kc == 0, kc == KC - 1,
                   [sls[i], mT], [ps], sig=(kc == KC - 1))
        CP(S, vx[:, blk, :], ps[:, :], [ps], [vx], eng="act")


def xattn(k, wq, wo):
    S = k.S
    kx, vx = k.kx, k.vx
    f256 = v3(16, 256)
    sc = 128.0 ** -0.5
    for hp in range(2):
        sl = wload(k, f256, wq[:, hp * 256:(hp + 1) * 256].rearrange("(kc p) c -> p kc c", p=128))
        for hi_ in range(2):
            hx = hp * 2 + hi_
            proj_fm(k, sl, f256(sl), hi_ * 128, [(k.hA, NT)], k.qf)
            headnorm(k, k.qf, 1024, k.G_XQ, k.qn)
            CP(S, k.qT[:, :], k.qn[:, :], [k.qn], [k.qT])
            for tt in range(2):
                Es = []
                for blk in range(2):
                    ps = k.psum.next()
                    MM(S, ps[:, :], kx[:, hx, blk * 128:(blk + 1) * 128], k.qT[:, tt * TT:(tt + 1) * TT], True, True, [kx, k.qT], [ps])
                    E = k.Ering.next()
                    ACT(S, E[:, :], ps[:, :], AF.Exp, [ps], [E], scale=sc)
                    Es.append(E)
                pso = k.psum.next()
                psd = k.psum.next()
                for blk in range(2):
                    MM(S, pso[:, :], vx[:, blk, hx * 128:(hx + 1) * 128], Es[blk][:, :], blk == 0, blk == 1, [vx, Es[blk]], [pso])
                for blk in range(2):
                    MM(S, psd[:, :], k.ones[:, :], Es[blk][:, :], blk == 0, blk == 1, [k.ones, Es[blk]], [psd])
                rd = k.rsring.next()
                recip_act(k, rd[:, :], psd[:, :], [psd], [rd])
                TT_(S, k.oT[:, hx, tt * TT:(tt + 1) * TT], pso[:, :], rd[:, :], ALU.mult, [pso, rd], [k.oT])
    add_proj(k, wo, 4, k.oT)


NG = 96
G_FFN1, G_MIX, G_XA, G_MEM, G_FFN2 = 0, 16, 32, 48, 64
G_AQ, G_AK, G_XQ, G_XK, G_GO, G_B2 = 80, 81, 82, 83, 84, 86
NMASK = 512


def wout_to_dram(k, w, src_bf, xin_d, xin_buf, xout_d, xout_buf):
    S = k.S
    xi = xin_d.rearrange("(kc p) n -> p kc n", p=128)
    xo = xout_d.rearrange("(kc p) n -> p kc n", p=128)
    fn = v3(16, 256)
    for mp in range(8):
        sl = wload(k, fn, w[:, mp * 256:(mp + 1) * 256].rearrange("(kc p) c -> p kc c", p=128))
        for mi in range(2):
            m = mp * 2 + mi
            pss = [k.psum.next() for _ in range(2)]
            for kc in range(16):
                for tt in range(2):
                    MM(S, pss[tt][:, :], fn(sl)[:, kc, mi * 128:(mi + 1) * 128], src_bf[:, kc, tt * TT:(tt + 1) * TT], kc == 0, kc == 15,
                       [sl, src_bf], [pss[tt]])
            for tt in range(2):
                xs = k.stage.next()
                S.dma("sp", xs[:, :], xi[:, m, tt * TT:(tt + 1) * TT], reads=[xin_buf], writes=[xs])
                TT_(S, xs[:, :], xs[:, :], pss[tt][:, :], ALU.add, [pss[tt], xs], [xs])
                S.dma("sp", xo[:, m, tt * TT:(tt + 1) * TT], xs[:, :], reads=[xs], writes=[xout_buf])


def build_program(stage=99, debug=False):
    nc = bass.Bass("TRN2", target_bir_lowering=False)

    def din(name, shape, dt=F32):
        return nc.dram_tensor(name, list(shape), dt, kind="ExternalInput").ap()

    xC_d = din("xC", [D, NT])
    xH_d = din("xH", [D, NT])
    memT_d = din("memT", [D, 256])
    W = {}
    for nm, shp in (("ffn1_w_gate", [D, DFF]), ("ffn1_w_up", [D, DFF]), ("ffn1_w_down", [DFF, D]), ("w_in", [D, 6160]),
                    ("w_out", [D, D]), ("xattn_w_q", [D, 512]), ("xattn_w_k", [D, 512]), ("xattn_w_v", [D, 512]),
                    ("xattn_w_o", [512, D]), ("ffn2_w_gate", [D, DFF]), ("ffn2_w_up", [D, DFF]), ("ffn2_w_down", [DFF, D])):
        W[nm] = din(nm, shp)
    gains_d = din("gains", [128, NG])
    w2_d = din("w2aug", [17, 512])
    cos_d = din("cos2", [128, 2048])
    sin_d = din("sin2", [128, 2048])
    masks_d = din("masks", [128, NMASK + 128])
    mats_d = din("mats", [128, 256])
    out_d = nc.dram_tensor("outT", [D, NT], F32, kind="ExternalOutput").ap()
    x1_d = nc.dram_tensor("x1_scratch", [D, NT], F32, kind="ExternalOutput").ap()
    x2_d = nc.dram_tensor("x2_scratch", [D, NT], F32, kind="ExternalOutput").ap()

    with ExitStack() as st:
        S = Sched(nc, st)
        k = K()
        k.S = S
        k.nc = nc
        k.G_AQ, k.G_AK, k.G_XQ, k.G_XK, k.G_GO, k.G_MEM = G_AQ, G_AK, G_XQ, G_XK, G_GO, G_MEM
        k.psum = Rot([S.ps("ps%d" % i, [128, 512]) for i in range(8)])
        k.gains = S.sb("gains", [128, NG], F32)
        k.ones = S.sb("ones", [128, 128], BF16)
        k.onecol = S.sb("onecol", [128, 1], F32)
        k.masks = S.sb("masks", [128, NMASK + 128], BF16)
        k.M_same = k.masks[:, 0:128]
        k.M_next = k.masks[:, 128:256]
        k.M_nextC = k.masks[:, 256:384]
        k.M16 = k.masks[:, 384:448]
        k.MG = k.masks[:, NMASK:NMASK + 128]
        k.mats = S.sb("mats", [128, 256], F32)
        k.ident = k.mats[:, 0:128]
        k.perm = k.mats[:, 128:256]
        k.w2 = S.sb("w2", [32, 512], BF16)
        k.hA = S.sb("hA", [128, KC, NT], BF16)
        k.wring = Rot([S.sb("wr%d" % i, [128, 4096], BF16) for i in range(4)])
        k.sqring = Rot([S.sb("sq%d" % i, [128, TT], BF16) for i in range(5)])
        k.rsring = Rot([S.sb("rs%d" % i, [128, TT], F32) for i in range(8)])
        x1b = S.dram("x1d", x1_d)
        x2b = S.dram("x2d", x2_d)
        outb = S.dram("out", out_d)

        S.dma("sp", k.gains[:, :], gains_d, writes=[k.gains])
        S.dma("sp", k.mats[:, :], mats_d, writes=[k.mats])
        S.dma("pool", k.masks[:, :], masks_d, writes=[k.masks])
        S.op("dve", lambda e: e.memset(k.w2[:, :], 0.0), [], [k.w2])
        S.dma("pool", k.w2[0:17, :], w2_d, writes=[k.w2])
        S.op("dve", lambda e: e.memset(k.ones[:, :], 1.0), [], [k.ones])
        k.permb = S.sb("permb", [128, 128], BF16)
        CP(S, k.permb[:, :], k.perm, [k.mats], [k.permb])
        S.op("dve", lambda e: e.memset(k.onecol[:, :], 1.0), [], [k.onecol])
        k.epscol = S.sb("epscol", [128, 1], F32)
        S.op("dve", lambda e: e.memset(k.epscol[:, :], EPS), [], [k.epscol])

        def alloc_x(ph, tag):
            k.xT = [[S.sb("x%s%d_%d" % (tag, kc, tt), [128, TT], F32, ph) for tt in range(2)] for kc in range(KC)]

        def alloc_ffn(ph, tag):
            k.aT = Rot([S.sb("aT%s%d" % (tag, i), [128, 4, NT], BF16, ph) for i in range(1)])
            k.sgring = Rot([S.sb("sg%s%d" % (tag, i), [128, TT], F32, ph) for i in range(2)])

        def store_x(dst, dstbuf):
            dv = dst.rearrange("(kc p) n -> p kc n", p=128)
            for kc in range(KC):
                for tt in range(2):
                    S.dma("sp", dv[:, kc, tt * TT:(tt + 1) * TT], k.xT[kc][tt][:, :], reads=[k.xT[kc][tt]], writes=[dstbuf])

        def finish(ph_note=""):
            S.barrier()
            S.emit()

        with ExitStack() as ph12:
            k.hC = S.sb("hC", [128, KC, NT], BF16, ph12)
            with ExitStack() as ph:
                alloc_x(ph, "a")
                alloc_ffn(ph, "a")
                xf, xb = x_fn(k)
                for which in range(2):
                    load_x(k, xC_d if which == 0 else xH_d)
                    if DBG.get('p1') == 'ls':
                        continue
                    rmsnorm_fm(k, xf, xb, G_FFN1, *h_fn(k.hA), NT)
                    if DBG.get('p1') == 'n1':
                        continue
                    ffn(k, W["ffn1_w_gate"], W["ffn1_w_up"], W["ffn1_w_down"])
                    tgt = k.hC if which == 0 else k.hA
                    rmsnorm_fm(k, xf, xb, G_MIX, *h_fn(tgt), NT)
                store_x(x1_d, x1b)
                if stage == 1:
                    store_x(out_d, outb)
                finish()
            if stage == 1:
                return nc

            with ExitStack() as ph:
                k.oT = S.sb("oT", [128, 16, NT], BF16, ph)
                k.qT = S.sb("qT", [128, 1024], BF16, ph)
                Pbase = [S.sb("P%d" % i, [128, 256], BF16, ph) for i in range(8)]
                with ExitStack() as pa:
                    k.cos2 = S.sb("cos2", [128, 2048], F32, pa)
                    k.sin2 = S.sb("sin2", [128, 2048], F32, pa)
                    S.dma("sp", k.cos2[:, :], cos_d, writes=[k.cos2])
                    S.dma("sp", k.sin2[:, :], sin_d, writes=[k.sin2])
                    k.kT = S.sb("kT", [128, 2048], BF16, pa)
                    k.Pring = Rot(Pbase + [S.sb("Px%d" % i, [128, 256], BF16, pa) for i in range(2)])
                    k.V1 = S.sb("V1", [128, 9, 128], BF16, pa)
                    k.V4 = S.sb("V4", [128, 12, 128], BF16, pa)
                    k.V16 = S.sb("V16", [128, 16, 128], BF16, pa)
                    k.OD = S.sb("OD", [128, 2, 1024], F32, pa)
                    k.vT = S.sb("vT", [128, 2048], F32, pa)
                    k.Ering = Rot([S.sb("E%d" % i, [128, 256], BF16, pa) for i in range(4)])
                    for h in range(8):
                        attn_head(k, h, W["w_in"], pa)
                    finish()
                pg = ph
                k.Pring = Rot(Pbase)
                k.glrT = S.sb("glrT", [32, 2048], BF16, pg)
                k.kf = S.sb("kf", [128, 2048], F32, pg)
                k.lbuf = S.sb("lbuf", [128, 2048], F32, pg)
                k.cl = S.sb("cl", [128, 2048], F32, pg)
                k.rmask = S.sb("rmask", [128, TT], F32, pg)
                k.keT = S.sb("keT", [128, 1024], BF16, pg)
                k.dtot = S.sb("dtot", [128, 32], F32, pg)
                k.og = S.sb("og", [128, 2, 1024], F32, pg)
                k.Sfr = Rot([S.sb("Sf%d" % i, [128, 256], F32, pg) for i in range(4)])
                k.Sbr = Rot([S.sb("Sb%d" % i, [128, 256], BF16, pg) for i in range(4)])
                k.vtring = Rot([S.sb("vt%d" % i, [128, 256], BF16, pg) for i in range(3)])
                k.kdring = Rot([S.sb("kd%d" % i, [128, 128], BF16, pg) for i in range(3)])
                k.stage = k.rsring
                S.op("dve", lambda e: e.memset(k.rmask[:, :], 1.0), [], [k.rmask])
                S.op("dve", lambda e: e.memset(k.rmask[:, 0:TT:64], 0.0), [k.rmask], [k.rmask])
                gla_prep(k, W["w_in"])
                for g in range(4):
                    gla_head(k, g, W["w_in"])
                if stage == 2:
                    ob = out_d.rearrange("(kc p) n -> p kc n", p=128)
                    for c in range(16):
                        xs = k.stage.next()
                        for tt in range(2):
                            CP(S, xs[:, :], k.oT[:, c, tt * TT:(tt + 1) * TT], [k.oT], [xs])
                            S.dma("sp", ob[:, c, tt * TT:(tt + 1) * TT], xs[:, :], reads=[xs], writes=[outb])
                    finish()
                    return nc
                wout_to_dram(k, W["w_out"], k.oT, x1_d, x1b, x2_d, x2b)
                finish()

        with ExitStack() as ph3:
            k.kx = S.sb("kx", [128, 4, 256], BF16, ph3)
            k.vx = S.sb("vx", [128, 2, 512], BF16, ph3)
            alloc_x(ph3, "b")
            xf, xb = x_fn(k)
            with ExitStack() as ph:
                k.memf = S.sb("memf", [128, KC, 256], F32, ph)
                k.mT = S.sb("mT", [128, KC, 256], BF16, ph)
                k.kf = S.sb("kfx", [128, 256], F32, ph)
                k.kn = S.sb("knx", [128, 256], F32, ph)
                k.qf = S.sb("qfx", [128, 1024], F32, ph)
                k.qn = S.sb("qnx", [128, 1024], F32, ph)
                k.qT = S.sb("qTx", [128, 1024], BF16, ph)
                k.Ering = Rot([S.sb("Ex%d" % i, [128, 512], BF16, ph) for i in range(4)])
                k.oT = S.sb("oTx", [128, 4, NT], BF16, ph)
                S.dma("sp", k.memf[:, :, :], memT_d.rearrange("(kc p) n -> p kc n", p=128), writes=[k.memf])
                load_x(k, x2_d)
                xattn_mem(k, W["xattn_w_k"], W["xattn_w_v"], memT_d)
                rmsnorm_fm(k, xf, xb, G_XA, *h_fn(k.hA), NT)
                xattn(k, W["xattn_w_q"], W["xattn_w_o"])
                if stage == 4:
                    store_x(out_d, outb)
                    finish()
                    return nc
                finish()
            with ExitStack() as ph:
                alloc_ffn(ph, "b")
                rmsnorm_fm(k, xf, xb, G_FFN2, *h_fn(k.hA), NT)
                ffn(k, W["ffn2_w_gate"], W["ffn2_w_up"], W["ffn2_w_down"])
                store_x(out_d, outb)
                finish()
    return nc


def _fm(a):
    return np.ascontiguousarray(np.asarray(a, dtype=np.float32).T)


def _col(v):
    v = np.asarray(v, dtype=np.float32).reshape(-1, 128)
    return np.ascontiguousarray(v.T)


def _consts(half):
    p = np.arange(128)
    inv = (10000.0 ** (-(np.arange(64, dtype=np.float32)) / 64.0)).astype(np.float32)
    pos = np.arange(2048, dtype=np.float32) - (0.0 if half == 1 else 1024.0)
    ang = pos[None, :].astype(np.float32) * inv[p % 64][:, None]
    cos2 = np.cos(ang).astype(np.float32)
    sin2 = np.sin(ang).astype(np.float32)
    sin2[:64] *= -1.0
    kk = p[:, None]
    qq = p[None, :]
    valid = 1.0 if half == 1 else 0.0
    m_same = (kk <= qq).astype(np.float32)
    m_next = (kk >= qq).astype(np.float32)
    m_nextc = m_next * valid
    q16 = 64 + np.arange(64)[None, :]
    m16 = np.where(kk < 64, valid, (kk <= q16).astype(np.float32)).astype(np.float32)
    m16 = np.broadcast_to(m16, (128, 64))
    pad = np.zeros((128, 64), np.float32)
    mg = ((kk // 64 == qq // 64) & (kk <= qq)).astype(np.float32)
    masks = np.concatenate([m_same, m_next, m_nextc, m16, pad, mg], axis=1).astype(np.float32)
    ident = np.eye(128, dtype=np.float32)
    perm = np.zeros((128, 128), np.float32)
    perm[(p + 64) % 128, p] = 1.0
    mats = np.concatenate([ident, perm], axis=1)
    return cos2, sin2, np.ascontiguousarray(masks), np.ascontiguousarray(mats)


_PROG = {}


def kernel(x, mem, ffn1_norm, ffn1_w_gate, ffn1_w_up, ffn1_w_down, mix_norm, w_in,
           attn_q_norm, attn_k_norm, gla_w_gate2, gla_b_gate2, gla_out_norm, w_out,
           xattn_norm, mem_norm, xattn_w_q, xattn_w_k, xattn_w_v, xattn_q_norm, xattn_k_norm,
           xattn_w_o, ffn2_norm, ffn2_w_gate, ffn2_w_up, ffn2_w_down, _stage=99, _cores=None):
    x = np.asarray(x, np.float32)
    mem = np.asarray(mem, np.float32)
    f = lambda a: np.ascontiguousarray(np.asarray(a, np.float32)[0])
    gains = np.zeros((128, NG), np.float32)
    gains[:, G_FFN1:G_FFN1 + 16] = _col(f(ffn1_norm))
    gains[:, G_MIX:G_MIX + 16] = _col(f(mix_norm))
    gains[:, G_XA:G_XA + 16] = _col(f(xattn_norm))
    gains[:, G_MEM:G_MEM + 16] = _col(f(mem_norm))
    gains[:, G_FFN2:G_FFN2 + 16] = _col(f(ffn2_norm))
    gains[:, G_AQ] = f(attn_q_norm)
    gains[:, G_AK] = f(attn_k_norm)
    gains[:, G_XQ] = f(xattn_q_norm)
    gains[:, G_XK] = f(xattn_k_norm)
    gains[:, G_GO:G_GO + 2] = _col(f(gla_out_norm))
    w2aug = np.ascontiguousarray(np.concatenate([f(gla_w_gate2), f(gla_b_gate2)[None, :]], axis=0))
    common = {
        "ffn1_w_gate": f(ffn1_w_gate), "ffn1_w_up": f(ffn1_w_up), "ffn1_w_down": f(ffn1_w_down), "w_in": f(w_in),
        "w_out": f(w_out), "xattn_w_q": f(xattn_w_q), "xattn_w_k": f(xattn_w_k), "xattn_w_v": f(xattn_w_v),
        "xattn_w_o": f(xattn_w_o), "ffn2_w_gate": f(ffn2_w_gate), "ffn2_w_up": f(ffn2_w_up), "ffn2_w_down": f(ffn2_w_down),
        "gains": gains, "w2aug": w2aug,
    }
    B = x.shape[0]
    jobs = [(b, s) for b in range(B) for s in range(2)]
    if _cores is not None:
        jobs = jobs[:_cores]
    consts = {s: _consts(s) for s in (0, 1)}
    in_maps = []
    for (b, s) in jobs:
        cos2, sin2, masks, mats = consts[s]
        xh = _fm(x[b, s * NT:(s + 1) * NT])
        xc = _fm(x[b, 0:NT]) if s == 1 else np.zeros((D, NT), np.float32)
        m = dict(common)
        m.update({"xC": xc, "xH": xh, "memT": _fm(mem[b]), "cos2": cos2, "sin2": sin2, "masks": masks, "mats": mats})
        in_maps.append(m)
    key = _stage
    if key not in _PROG:
        _PROG[key] = build_program(stage=_stage)
    res = run_bass_kernel_spmd(_PROG[key], in_maps, core_ids=list(range(len(jobs))))
    out = np.zeros((B, 2 * NT, D), np.float32)
    for (b, s), r in zip(jobs, res.results):
        out[b, s * NT:(s + 1) * NT] = np.asarray(r["outT"], np.float32).T
    return out
```
